# Optimizing a Trainium2 kernel written in Bass

```python
import jax, jax.numpy as jnp
from jax import lax
import numpy as np

D_MODEL = 1024
BATCH = 16
SEQ = 256
DEPTH = 2
DEC_BATCH = 2
DEC_SEQ = 2048
PAST_LEN = 512

GRID_W = 64
ROPE_THETA = 10000.0
NORM_EPS = 1e-6
Q_BLOCK = 128
WINDOW = 128
NEG_INF = -1e30

HEAD_DIM = 64
N_HEADS_A = 8
N_KV_A = 2
N_HEADS_B = 8
Q_LORA = 384
KV_LORA = 256
QK_NOPE = 64
QK_ROPE = 32
V_DIM_B = 64
N_HEADS_C = 16
N_KV_C = 2
D_FF = 2816
CONV_W = 3

N_EVEN = (DEPTH + 1) // 2
N_ODD = DEPTH // 2
IN_E = N_HEADS_A * HEAD_DIM + 2 * N_KV_A * HEAD_DIM + Q_LORA + KV_LORA + QK_ROPE
SPLIT_E = [N_HEADS_A * HEAD_DIM,
           N_HEADS_A * HEAD_DIM + N_KV_A * HEAD_DIM,
           N_HEADS_A * HEAD_DIM + 2 * N_KV_A * HEAD_DIM,
           N_HEADS_A * HEAD_DIM + 2 * N_KV_A * HEAD_DIM + Q_LORA,
           N_HEADS_A * HEAD_DIM + 2 * N_KV_A * HEAD_DIM + Q_LORA + KV_LORA]
MIX_E = N_HEADS_A * HEAD_DIM + N_HEADS_B * V_DIM_B
IN_O = N_HEADS_C * HEAD_DIM + 2 * N_KV_C * HEAD_DIM
SPLIT_O = [N_HEADS_C * HEAD_DIM, N_HEADS_C * HEAD_DIM + N_KV_C * HEAD_DIM]
MIX_O = N_HEADS_C * HEAD_DIM

kernel_name = "hybrid_diffusion_prefix_ctx_step"


def rmsnorm(x, g):
    xf = x.astype(jnp.float32)
    y = xf * lax.rsqrt(jnp.mean(xf * xf, axis=-1, keepdims=True) + NORM_EPS)
    return (y * g.astype(jnp.float32)).astype(x.dtype)


def axial_rope_tables(n_tokens, rot_dim):
    rows = n_tokens // GRID_W
    row = jnp.repeat(jnp.arange(rows), GRID_W).astype(jnp.float32)
    col = jnp.tile(jnp.arange(GRID_W), rows).astype(jnp.float32)
    d_axis = rot_dim // 2
    freqs = ROPE_THETA ** (-jnp.arange(0, d_axis, 2, dtype=jnp.float32) / d_axis)
    ang = jnp.concatenate([row[:, None] * freqs, col[:, None] * freqs], axis=-1)
    return jnp.cos(ang), jnp.sin(ang)


def apply_rope(x, cos, sin):
    x1 = x[..., 0::2].astype(jnp.float32)
    x2 = x[..., 1::2].astype(jnp.float32)
    c = cos[None, :, None, :]
    s = sin[None, :, None, :]
    out = jnp.stack([x1 * c - x2 * s, x1 * s + x2 * c], axis=-1).reshape(x.shape)
    return out.astype(x.dtype)


def modulation(cond, w_mod, b_mod):
    m = jax.nn.silu(cond) @ w_mod + b_mod
    return jnp.split(m[..., None, :], 6, axis=-1)


def softmax_attend(q, k, v, scale, bias=None, sink=None):
    s = jnp.einsum('bqhgd,bthd->bhgqt', q, k, preferred_element_type=jnp.float32) * scale
    if bias is not None:
        s = s + bias
    if sink is not None:
        sk = jnp.broadcast_to(sink.astype(jnp.float32)[None, :, :, None, None], s.shape[:-1] + (1,))
        p = jax.nn.softmax(jnp.concatenate([s, sk], axis=-1), axis=-1)[..., :-1]
    else:
        p = jax.nn.softmax(s, axis=-1)
    return jnp.einsum('bhgqt,bthd->bqhgd', p.astype(v.dtype), v)


def dense_attention_blocked(q, k, v, scale, sink=None):
    B, S, H, dk = q.shape
    hkv = k.shape[2]
    g = H // hkv
    nb = S // Q_BLOCK
    qb = q.reshape(B, nb, Q_BLOCK, hkv, g, dk).transpose(1, 0, 2, 3, 4, 5)
    ob = lax.map(lambda qq: softmax_attend(qq, k, v, scale, sink=sink), qb)
    return ob.transpose(1, 0, 2, 3, 4, 5).reshape(B, S, H, v.shape[-1])


def window_attention_with_context(q, k, v, k_ctx, v_ctx, scale, sink):
    B, S, H, dk = q.shape
    hkv = k.shape[2]
    g = H // hkv
    nb = S // Q_BLOCK
    band = Q_BLOCK + 2 * WINDOW
    kp = jnp.pad(k, ((0, 0), (WINDOW, WINDOW), (0, 0), (0, 0)))
    vp = jnp.pad(v, ((0, 0), (WINDOW, WINDOW), (0, 0), (0, 0)))
    qb = q.reshape(B, nb, Q_BLOCK, hkv, g, dk).transpose(1, 0, 2, 3, 4, 5)
    qi = jnp.arange(Q_BLOCK)[:, None]
    kj = jnp.arange(band)[None, :]
    rel = kj - WINDOW - qi
    ctx_bias = jnp.zeros((Q_BLOCK, k_ctx.shape[1]), jnp.float32)

    def block(args):
        b, qq = args
        start = b * Q_BLOCK
        kb = lax.dynamic_slice_in_dim(kp, start, band, axis=1)
        vb = lax.dynamic_slice_in_dim(vp, start, band, axis=1)
        kpos = start - WINDOW + kj
        ok = (jnp.abs(rel) <= WINDOW) & (kpos >= 0) & (kpos < S)
        bias = jnp.concatenate([jnp.where(ok, 0.0, NEG_INF).astype(jnp.float32), ctx_bias], axis=-1)
        return softmax_attend(qq, jnp.concatenate([kb, k_ctx], axis=1),
                              jnp.concatenate([vb, v_ctx], axis=1), scale, bias=bias, sink=sink)

    ob = lax.map(block, (jnp.arange(nb), qb))
    return ob.transpose(1, 0, 2, 3, 4, 5).reshape(B, S, H, v.shape[-1])


def even_mixer(h, w_in, g_qn, g_kn, g_cq, w_uq, g_ckv, w_ukv, w_out, rope, ctx):
    B, T, _ = h.shape
    q_a, k_a, v_a, cq, ckv, kpe = jnp.split(h @ w_in, SPLIT_E, axis=-1)
    q_a = rmsnorm(q_a.reshape(B, T, N_HEADS_A, HEAD_DIM), g_qn)
    k_a = rmsnorm(k_a.reshape(B, T, N_KV_A, HEAD_DIM), g_kn)
    v_a = v_a.reshape(B, T, N_KV_A, HEAD_DIM)
    q_b = (rmsnorm(cq, g_cq) @ w_uq).reshape(B, T, N_HEADS_B, QK_NOPE + QK_ROPE)
    q_b_nope, q_b_pe = q_b[..., :QK_NOPE], q_b[..., QK_NOPE:]
    ckv = rmsnorm(ckv, g_ckv)
    if rope is not None:
        cos_a, sin_a, cos_b, sin_b = rope
        q_a = apply_rope(q_a, cos_a, sin_a)
        k_a = apply_rope(k_a, cos_a, sin_a)
        q_b_pe = apply_rope(q_b_pe, cos_b, sin_b)
        kpe = apply_rope(kpe[:, :, None, :], cos_b, sin_b)[:, :, 0, :]
    own = (k_a, v_a, ckv, kpe)
    if ctx is None:
        k_all, v_all, ckv_all, kpe_all = own
    else:
        k_all = jnp.concatenate([k_a, ctx[0]], axis=1)
        v_all = jnp.concatenate([v_a, ctx[1]], axis=1)
        ckv_all = jnp.concatenate([ckv, ctx[2]], axis=1)
        kpe_all = jnp.concatenate([kpe, ctx[3]], axis=1)
    o_a = dense_attention_blocked(q_a, k_all, v_all, HEAD_DIM ** -0.5)
    tk = ckv_all.shape[1]
    kv = (ckv_all @ w_ukv).reshape(B, tk, N_HEADS_B, QK_NOPE + V_DIM_B)
    k_b = jnp.concatenate([kv[..., :QK_NOPE],
                           jnp.broadcast_to(kpe_all[:, :, None, :], (B, tk, N_HEADS_B, QK_ROPE))], axis=-1)
    v_b = kv[..., QK_NOPE:]
    o_b = dense_attention_blocked(jnp.concatenate([q_b_nope, q_b_pe], axis=-1), k_b, v_b,
                                  (QK_NOPE + QK_ROPE) ** -0.5)
    o = jnp.concatenate([o_a.reshape(B, T, -1), o_b.reshape(B, T, -1)], axis=-1) @ w_out
    return o, own


def odd_mixer(h, w_in, sink, w_out, rope, ctx):
    B, T, _ = h.shape
    q, k, v = jnp.split(h @ w_in, SPLIT_O, axis=-1)
    q = q.reshape(B, T, N_HEADS_C, HEAD_DIM)
    k = k.reshape(B, T, N_KV_C, HEAD_DIM)
    v = v.reshape(B, T, N_KV_C, HEAD_DIM)
    if rope is not None:
        cos_c, sin_c = rope
        q = apply_rope(q, cos_c, sin_c)
        k = apply_rope(k, cos_c, sin_c)
    sk = sink.reshape(N_KV_C, N_HEADS_C // N_KV_C)
    if ctx is None:
        o = dense_attention_blocked(q, k, v, HEAD_DIM ** -0.5, sink=sk)
    else:
        o = window_attention_with_context(q, k, v, ctx[0], ctx[1], HEAD_DIM ** -0.5, sk)
    return o.reshape(B, T, -1) @ w_out, (k, v)


def conv_ffn(h, w_up, conv_w, conv_b, w_down):
    gate, val = jnp.split(h @ w_up, 2, axis=-1)
    gp = jnp.pad(gate, ((0, 0), (1, 1), (0, 0)))
    gate = gp[:, :-2] * conv_w[0] + gp[:, 1:-1] * conv_w[1] + gp[:, 2:] * conv_w[2] + conv_b
    return (jax.nn.silu(gate) * val) @ w_down


def trunk(x, cond, W, rope_e, rope_o, cache):
    states = ([], [], [], [], [], [])
    for l in range(DEPTH):
        sh1, sc1, g1, sh2, sc2, g2 = modulation(cond, W['w_mod'][l], W['b_mod'][l])
        h = rmsnorm(x, W['g_mix_norm'][l]) * (1 + sc1) + sh1
        if l % 2 == 0:
            e = l // 2
            ctx = None if cache is None else (cache[0][:, e], cache[1][:, e], cache[2][:, e], cache[3][:, e])
            o, own = even_mixer(h, W['w_in_e'][e], W['g_qnorm_a'][e], W['g_knorm_a'][e], W['g_cq_b'][e],
                                W['w_uq_b'][e], W['g_ckv_b'][e], W['w_ukv_b'][e], W['w_out_e'][e], rope_e, ctx)
            if cache is None:
                for s, t in zip(states[:4], own):
                    s.append(t)
        else:
            od = l // 2
            ctx = None if cache is None else (cache[4][:, od], cache[5][:, od])
            o, own = odd_mixer(h, W['w_in_o'][od], W['sink_c'][od], W['w_out_o'][od], rope_o, ctx)
            if cache is None:
                for s, t in zip(states[4:], own):
                    s.append(t)
        x = x + g1 * o
        h = rmsnorm(x, W['g_ffn_norm'][l]) * (1 + sc2) + sh2
        x = x + g2 * conv_ffn(h, W['w_up'][l], W['conv_w'][l], W['conv_b'][l], W['w_down'][l])
    return rmsnorm(x, W['g_final']), states


def setup_inputs(seed: int = 0) -> dict:
    key = jax.random.key(seed)
    ks = jax.random.split(key, 32)

    def nrm(k, shape, scale=1.0):
        return jax.random.normal(k, shape, jnp.float32) * scale

    def gain(k, shape):
        return 1.0 + 0.1 * jax.random.normal(k, shape, jnp.float32)

    return {
        'x_prompt': nrm(ks[0], (BATCH, SEQ, D_MODEL)),
        'x_sample': nrm(ks[1], (DEC_BATCH, DEC_SEQ, D_MODEL)),
        'cache_a_k': nrm(ks[2], (DEC_BATCH, N_EVEN, PAST_LEN, N_KV_A, HEAD_DIM)),
        'cache_a_v': nrm(ks[3], (DEC_BATCH, N_EVEN, PAST_LEN, N_KV_A, HEAD_DIM)),
        'cache_b_ckv': nrm(ks[4], (DEC_BATCH, N_EVEN, PAST_LEN, KV_LORA)),
        'cache_b_kpe': nrm(ks[5], (DEC_BATCH, N_EVEN, PAST_LEN, QK_ROPE)),
        'cache_c_k': nrm(ks[6], (DEC_BATCH, N_ODD, PAST_LEN, N_KV_C, HEAD_DIM)),
        'cache_c_v': nrm(ks[7], (DEC_BATCH, N_ODD, PAST_LEN, N_KV_C, HEAD_DIM)),
        'c': nrm(ks[8], (DEC_BATCH, D_MODEL)),
        'c_ctx': nrm(ks[9], (D_MODEL,)),
        'w_mod': nrm(ks[10], (DEPTH, D_MODEL, 6 * D_MODEL), 0.5 * D_MODEL ** -0.5),
        'b_mod': nrm(ks[11], (DEPTH, 6 * D_MODEL), 0.01),
        'g_mix_norm': gain(ks[12], (DEPTH, D_MODEL)),
        'g_ffn_norm': gain(ks[13], (DEPTH, D_MODEL)),
        'w_in_e': nrm(ks[14], (N_EVEN, D_MODEL, IN_E), D_MODEL ** -0.5),
        'g_qnorm_a': gain(ks[15], (N_EVEN, HEAD_DIM)),
        'g_knorm_a': gain(ks[16], (N_EVEN, HEAD_DIM)),
        'g_cq_b': gain(ks[17], (N_EVEN, Q_LORA)),
        'w_uq_b': nrm(ks[18], (N_EVEN, Q_LORA, N_HEADS_B * (QK_NOPE + QK_ROPE)), Q_LORA ** -0.5),
        'g_ckv_b': gain(ks[19], (N_EVEN, KV_LORA)),
        'w_ukv_b': nrm(ks[20], (N_EVEN, KV_LORA, N_HEADS_B * (QK_NOPE + V_DIM_B)), KV_LORA ** -0.5),
        'w_out_e': nrm(ks[21], (N_EVEN, MIX_E, D_MODEL), MIX_E ** -0.5),
        'w_in_o': nrm(ks[22], (N_ODD, D_MODEL, IN_O), D_MODEL ** -0.5),
        'sink_c': nrm(ks[23], (N_ODD, N_HEADS_C)),
        'w_out_o': nrm(ks[24], (N_ODD, MIX_O, D_MODEL), MIX_O ** -0.5),
        'w_up': nrm(ks[25], (DEPTH, D_MODEL, 2 * D_FF), D_MODEL ** -0.5),
        'conv_w': nrm(ks[26], (DEPTH, CONV_W, D_FF), CONV_W ** -0.5),
        'conv_b': nrm(ks[27], (DEPTH, D_FF), 0.01),
        'w_down': nrm(ks[28], (DEPTH, D_FF, D_MODEL), D_FF ** -0.5),
        'g_final': gain(ks[29], (D_MODEL,)),
    }


def reference(x_prompt, x_sample, cache_a_k, cache_a_v, cache_b_ckv, cache_b_kpe, cache_c_k, cache_c_v,
              c, c_ctx, w_mod, b_mod, g_mix_norm, g_ffn_norm, w_in_e, g_qnorm_a, g_knorm_a, g_cq_b,
              w_uq_b, g_ckv_b, w_ukv_b, w_out_e, w_in_o, sink_c, w_out_o, w_up, conv_w, conv_b,
              w_down, g_final):
    W = {'w_mod': w_mod, 'b_mod': b_mod, 'g_mix_norm': g_mix_norm, 'g_ffn_norm': g_ffn_norm,
         'w_in_e': w_in_e, 'g_qnorm_a': g_qnorm_a, 'g_knorm_a': g_knorm_a, 'g_cq_b': g_cq_b,
         'w_uq_b': w_uq_b, 'g_ckv_b': g_ckv_b, 'w_ukv_b': w_ukv_b, 'w_out_e': w_out_e,
         'w_in_o': w_in_o, 'sink_c': sink_c, 'w_out_o': w_out_o, 'w_up': w_up, 'conv_w': conv_w,
         'conv_b': conv_b, 'w_down': w_down, 'g_final': g_final}

    y_prompt, st = trunk(x_prompt, c_ctx, W, None, None, None)
    new_a_k = jnp.stack(st[0], axis=1)
    new_a_v = jnp.stack(st[1], axis=1)
    new_b_ckv = jnp.stack(st[2], axis=1)
    new_b_kpe = jnp.stack(st[3], axis=1)
    new_c_k = jnp.stack(st[4], axis=1)
    new_c_v = jnp.stack(st[5], axis=1)

    n_lat = x_sample.shape[1]
    cos_a, sin_a = axial_rope_tables(n_lat, HEAD_DIM)
    cos_b, sin_b = axial_rope_tables(n_lat, QK_ROPE)
    cache = (cache_a_k, cache_a_v, cache_b_ckv, cache_b_kpe, cache_c_k, cache_c_v)
    y_sample, _ = trunk(x_sample, c, W, (cos_a, sin_a, cos_b, sin_b), (cos_a, sin_a), cache)

    return (y_prompt, y_sample, new_a_k, new_a_v, new_b_ckv, new_b_kpe, new_c_k, new_c_v)
```

```python
import os
from contextlib import ExitStack

import numpy as np
import concourse.bass as bass
import concourse.mybir as mybir
from concourse.bass_utils import run_bass_kernel_spmd

F32 = mybir.dt.float32
BF16 = mybir.dt.bfloat16
U8 = mybir.dt.uint8
AF = mybir.ActivationFunctionType
ALU = mybir.AluOpType
AX = mybir.AxisListType

EPS = 1e-6
NCORES = 8
TP = 512
SEQ = 256
HALO = 130
TS = 512 + 2 * HALO
Q1OFF = 129
TQ1 = 514
K1OFF = 1
K1T = 110
NK1T = 7
OWNOFF = 130
NFULL = 2048
NCTX = 512
NK0 = NFULL + NCTX
DFF = 2816
NJ = 22
SEM_LIMIT = 20000

STAGE = int(os.environ.get("KSTAGE", "99"))


def cdiv(a, b):
    return (a + b - 1) // b


def nchunks(T, maxn=512):
    n = cdiv(T, maxn)
    sz = cdiv(T, n)
    return [(i * sz, min(T, (i + 1) * sz)) for i in range(n)]


def ttiles(T, tsz=128):
    return [(i, min(tsz, T - i)) for i in range(0, T, tsz)]


class Trk:
    __slots__ = ("w", "r")

    def __init__(self, inherit=None):
        self.w = None
        self.r = list(inherit) if inherit else []


def _dedupe(evs):
    d = {}
    for e in evs:
        k = id(e[0])
        if k not in d or d[k][1] < e[1]:
            d[k] = e
    return list(d.values())


class Eng:
    def __init__(self, fw, name, h):
        self.fw = fw
        self.name = name
        self.h = h
        self.sems = []
        self.cnt = 0
        self.seen = {}
        self.ninst = 0

    def cur_sem(self):
        if not self.sems or self.cnt >= SEM_LIMIT:
            self.sems.append(self.fw.new_sem("e_%s_%d" % (self.name, len(self.sems))))
            self.cnt = 0
        return self.sems[-1]


class FW:
    def __init__(self, nc, stack):
        self.nc = nc
        self.stack = stack
        self.nsem = 0
        self.pe = Eng(self, "pe", nc.tensor)
        self.act = Eng(self, "act", nc.scalar)
        self.dve = Eng(self, "dve", nc.vector)
        self.pool = Eng(self, "pool", nc.gpsimd)
        self.sp = Eng(self, "sp", nc.sync)
        self.dma_pools = {}
        self.out_events = []
        self.flip = 0

    def new_sem(self, name):
        self.nsem += 1
        return self.stack.enter_context(self.nc.semaphore(name))

    def _wait(self, eng, ev):
        sem, val = ev
        k = id(sem)
        if eng.seen.get(k, 0) >= val:
            return
        eng.seen[k] = val
        eng.h.wait_ge(sem, val)
        eng.ninst += 1

    def _skip(self, eng, ev):
        return eng is self.pe and any(ev[0] is s for s in eng.sems)

    def _deps(self, eng, reads, writes):
        for t in reads:
            if t.w is not None and not self._skip(eng, t.w):
                self._wait(eng, t.w)
        for t in writes:
            if t.w is not None and not self._skip(eng, t.w):
                self._wait(eng, t.w)
            for ev in t.r:
                if not self._skip(eng, ev):
                    self._wait(eng, ev)

    def _needs(self, eng, reads, writes, acc):
        def add(ev):
            if self._skip(eng, ev):
                return
            k = id(ev[0])
            if k not in acc or acc[k][1] < ev[1]:
                acc[k] = ev
        for t in reads:
            if t.w is not None:
                add(t.w)
        for t in writes:
            if t.w is not None:
                add(t.w)
            for ev in t.r:
                add(ev)

    def mm_burst(self, groups):
        eng = self.pe
        acc = {}
        for (out, mms, reads, writes) in groups:
            self._needs(eng, reads, writes, acc)
        for ev in acc.values():
            self._wait(eng, ev)
        evs = []
        for (out, mms, reads, writes) in groups:
            ins = None
            for (l, r, st_, sp_) in mms:
                ins = self.nc.tensor.matmul(out, l, r, start=st_, stop=sp_)
                eng.ninst += 1
            sem = eng.cur_sem()
            eng.cnt += 1
            ins.then_inc(sem, 1)
            ev = (sem, eng.cnt)
            self._mark(ev, reads, writes)
            evs.append(ev)
        return evs

    def _mark(self, ev, reads, writes):
        for t in reads:
            t.r.append(ev)
            if len(t.r) > 12:
                t.r = _dedupe(t.r)
        for t in writes:
            t.w = ev
            t.r = []

    def op(self, eng, fn, *args, reads=(), writes=(), **kw):
        self._deps(eng, reads, writes)
        ins = fn(*args, **kw)
        sem = eng.cur_sem()
        eng.cnt += 1
        ins.then_inc(sem, 1)
        eng.ninst += 1
        ev = (sem, eng.cnt)
        self._mark(ev, reads, writes)
        return ev

    def mm_group(self, out, pairs, reads, writes, tr=False):
        eng = self.pe
        self._deps(eng, reads, writes)
        n = len(pairs)
        ins = None
        for i, (l, r) in enumerate(pairs):
            ins = self.nc.tensor.matmul(out, l, r, start=(i == 0), stop=(i == n - 1))
            eng.ninst += 1
        sem = eng.cur_sem()
        eng.cnt += 1
        ins.then_inc(sem, 1)
        ev = (sem, eng.cnt)
        self._mark(ev, reads, writes)
        return ev

    def dma(self, q, out, in_, reads=(), writes=(), is_output=False):
        pool = self.dma_pools.setdefault(q.name, {"sems": [], "vals": [], "i": 0})
        NP = 12
        if len(pool["sems"]) < NP:
            pool["sems"].append(self.new_sem("d_%s_%d" % (q.name, len(pool["sems"]))))
            pool["vals"].append(0)
            i = len(pool["sems"]) - 1
        else:
            i = pool["i"] % NP
            self._wait(q, (pool["sems"][i], pool["vals"][i]))
        pool["i"] += 1
        self._deps(q, reads, writes)
        sem = pool["sems"][i]
        pool["vals"][i] += 16
        ins = q.h.dma_start(out=out, in_=in_)
        ins.then_inc(sem, 16)
        q.ninst += 1
        ev = (sem, pool["vals"][i])
        self._mark(ev, reads, writes)
        if is_output:
            self.out_events.append(ev)
        return ev

    def finish(self):
        for ev in _dedupe(self.out_events):
            self._wait(self.sp, ev)
        for e in (self.pe, self.act, self.dve, self.pool):
            if e.sems:
                self._wait(self.sp, (e.sems[-1], e.cnt))


class Buf:
    def __init__(self, arena, name, off, nbytes, ap, inherit):
        self.arena = arena
        self.name = name
        self.off = off
        self.nbytes = nbytes
        self.ap = ap
        self.inherit = inherit
        self.trks = {}

    def t(self, key=0):
        if key not in self.trks:
            self.trks[key] = Trk(self.inherit)
        return self.trks[key]

    def free(self):
        self.arena.free(self)


class Arena:
    def __init__(self, tensor, size):
        self.tensor = tensor
        self.size = size
        self.freelist = [(0, size)]
        self.dead = []
        self.peak = 0
        self.live = {}

    def alloc(self, name, free_shape, dtype):
        esz = {F32: 4, BF16: 2, U8: 1}[dtype]
        n = 1
        for s in free_shape:
            n *= s
        nbytes = cdiv(n * esz, 64) * 64
        for i, (a, b) in enumerate(self.freelist):
            if b - a >= nbytes:
                off = a
                if b - a == nbytes:
                    self.freelist.pop(i)
                else:
                    self.freelist[i] = (a + nbytes, b)
                break
        else:
            raise RuntimeError("SBUF arena OOM allocating %s (%d B); live=%s" % (
                name, nbytes, {k: v for k, v in self.live.items()}))
        inherit = []
        nd = []
        for (a, b, evs) in self.dead:
            if a < off + nbytes and off < b:
                inherit.extend(evs)
                if a < off:
                    nd.append((a, off, evs))
                if b > off + nbytes:
                    nd.append((off + nbytes, b, evs))
            else:
                nd.append((a, b, evs))
        self.dead = nd
        inherit = _dedupe(inherit)
        ap = self.tensor[:, off:off + n * esz]
        if dtype != U8:
            ap = ap.bitcast(dtype)
        if len(free_shape) == 2:
            ap = ap.rearrange("p (a b) -> p a b", a=free_shape[0])
        elif len(free_shape) == 3:
            ap = ap.rearrange("p (a b c) -> p a b c", a=free_shape[0], b=free_shape[1])
        elif len(free_shape) == 4:
            ap = ap.rearrange("p (a b c d) -> p a b c d", a=free_shape[0], b=free_shape[1], c=free_shape[2])
        self.live[name] = nbytes
        self.peak = max(self.peak, sum(self.live.values()))
        return Buf(self, name, off, nbytes, ap, inherit)

    def free(self, buf):
        evs = list(buf.inherit)
        for t in buf.trks.values():
            if t.w is not None:
                evs.append(t.w)
            evs.extend(t.r)
        self.dead.append((buf.off, buf.off + buf.nbytes, _dedupe(evs)))
        del self.live[buf.name]
        fl = self.freelist + [(buf.off, buf.off + buf.nbytes)]
        fl.sort()
        merged = []
        for a, b in fl:
            if merged and merged[-1][1] == a:
                merged[-1] = (merged[-1][0], b)
            else:
                merged.append((a, b))
        self.freelist = merged


def _rope_tables(positions, rot_dim):
    pos = np.asarray(positions, dtype=np.int64)
    ok = (pos >= 0) & (pos < NFULL)
    pc = np.where(ok, pos, 0)
    row = (pc // 64).astype(np.float64)
    col = (pc % 64).astype(np.float64)
    d_axis = rot_dim // 2
    freqs = 10000.0 ** (-np.arange(0, d_axis, 2, dtype=np.float64) / d_axis)
    ang = np.concatenate([row[:, None] * freqs, col[:, None] * freqs], axis=-1)
    out = np.concatenate([np.cos(ang), np.sin(ang)], axis=-1).astype(np.float32)
    out[~ok] = 0.0
    return out


def _tile_rows(a, tsz, ntile):
    n, f = a.shape
    out = np.zeros((128, ntile, f), np.float32)
    for t in range(ntile):
        lo = t * tsz
        hi = min(n, lo + tsz)
        if hi > lo:
            out[:hi - lo, t, :] = a[lo:hi]
    return out


def _fm(vec, nch):
    return np.ascontiguousarray(np.asarray(vec, np.float32).reshape(nch, 128).T)


def _bc(vec):
    return np.ascontiguousarray(np.broadcast_to(np.asarray(vec, np.float32)[None, :], (128, len(vec))))


def _const_layout():
    lay = {}
    off = 0

    def add(name, n):
        nonlocal off
        lay[name] = (off, off + n)
        off += n
    for l in range(2):
        add("gmix%d" % l, 8)
        add("gffn%d" % l, 8)
        add("bmod%d" % l, 48)
        add("convw%d" % l, NJ * 3)
        add("convb%d" % l, NJ)
    add("gfin", 8)
    add("condT", 16)
    add("sink", 16)
    add("gq", 64)
    add("gk", 64)
    add("gcq", 384)
    add("gckv", 256)
    add("validk1", NK1T)
    return lay, off


CL, NCONST = _const_layout()


def build_program():
    nc = bass.Bass("TRN2", target_bir_lowering=False)

    def din(name, shape):
        return nc.dram_tensor(name, list(shape), F32, kind="ExternalInput").ap()

    def dout(name, shape):
        return nc.dram_tensor(name, list(shape), F32, kind="ExternalOutput").ap()

    d_xp = din("xp", [TP, 1024])
    d_xs = din("xs", [TS, 1024])
    d_xf = din("xf", [NFULL, 1024])
    d_cak = din("ca_k", [NCTX, 128])
    d_cav = din("ca_v", [NCTX, 128])
    d_cbckv = din("cb_ckv", [NCTX, 256])
    d_cbkpe = din("cb_kpe", [NCTX, 32])
    d_cck = din("cc_k", [NCTX, 128])
    d_ccv = din("cc_v", [NCTX, 128])
    d_consts = din("consts", [128, NCONST])
    d_vmask = din("vmask", [128, TS])
    d_band = din("band", [128, NK1T, TQ1])
    d_ropeA_ext = din("ropeA_ext", [128, 7, 64])
    d_ropeB_ext = din("ropeB_ext", [128, 7, 32])
    d_ropeA_full = din("ropeA_full", [128, 16, 64])
    d_ropeB_full = din("ropeB_full", [128, 16, 32])
    d_ropeA_q1 = din("ropeA_q1", [128, 5, 64])
    d_ropeA_k1 = din("ropeA_k1", [128, NK1T, 64])
    d_wmod = din("w_mod_r", [2, 12, 128, 8, 512])
    d_wine = din("w_in_e", [1024, 1440])
    d_wuq = din("w_uq", [384, 768])
    d_wukv = din("w_ukv", [256, 1024])
    d_woute = din("w_out_e", [1024, 1024])
    d_wino = din("w_in_o", [1024, 1280])
    d_wouto = din("w_out_o", [1024, 1024])
    d_wup = din("w_up_r", [2, NJ, 128, 8, 256])
    d_wdn = din("w_down_r", [2, 8, 128, NJ, 128])

    o_yp = dout("y_p", [TP, 1024])
    o_ys = dout("y_s", [512, 1024])
    o_ak = dout("n_ak", [TP, 128])
    o_av = dout("n_av", [TP, 128])
    o_ckv = dout("n_ckv", [TP, 256])
    o_kpe = dout("n_kpe", [TP, 32])
    o_ck = dout("n_ck", [TP, 128])
    o_cv = dout("n_cv", [TP, 128])

    st = ExitStack()
    with st:
        fw = FW(nc, st)
        pe, act, dve, pool, sp = fw.pe, fw.act, fw.dve, fw.pool, fw.sp
        ARENA_BYTES = 207 * 1024
        arena_t = st.enter_context(nc.sbuf_tensor("arena", [128, ARENA_BYTES], U8))
        ar = Arena(arena_t, ARENA_BYTES)
        banks = []
        for i in range(8):
            banks.append((st.enter_context(nc.psum_tensor("bank%d" % i, [128, 512], F32)), Trk()))
        bstate = {"s": 0, "a": 0, "l0": 0, "l1": 0, "l2": 0, "l3": 0, "lanes": False, "li": 0}
        bgstate = {"gen": True, "done": 0}

        def bank(kind="s"):
            if kind == "bg":
                return banks[7]
            if kind == "a":
                i = 5 + bstate["a"] % 2
                bstate["a"] += 1
                return banks[i]
            if bstate["lanes"]:
                key = "l%d" % bstate["li"]
                i = 2 * bstate["li"] + bstate[key] % 2
                bstate[key] += 1
                return banks[i]
            spool = [0, 1, 2, 3, 4] + ([7] if bgstate["gen"] is None else [])
            i = spool[bstate["s"] % len(spool)]
            bstate["s"] += 1
            return banks[i]

        MOD_SPREAD = 3
        V = nc.vector
        A = nc.scalar
        G = nc.gpsimd

        consts = ar.alloc("consts", [NCONST], F32)
        fw.dma(sp, consts.ap, d_consts, writes=[consts.t()])

        def cst(name, a=None, b=None):
            lo, hi = CL[name]
            if a is None:
                return consts.ap[:, lo:hi]
            return consts.ap[:, lo + a:lo + b]

        ident = ar.alloc("ident", [128], F32)
        identb = ar.alloc("identb", [128], BF16)
        onesb = ar.alloc("onesb", [128], BF16)
        fw.op(pool, G.memset, ident.ap, 0.0, writes=[ident.t()])
        fw.op(pool, G.affine_select, ident.ap, ident.ap, [[-1, 128]], ALU.not_equal, 1.0, base=0,
              channel_multiplier=1, reads=[ident.t()], writes=[ident.t()])
        fw.op(pool, G.tensor_copy, identb.ap, ident.ap, reads=[ident.t()], writes=[identb.t()])
        fw.op(pool, G.memset, onesb.ap, 1.0, writes=[onesb.t()])
        epsb = ar.alloc("epsb", [16], F32)
        fw.op(pool, G.memset, epsb.ap, EPS, writes=[epsb.t()])

        class _Lane:
            pass
        NLANE = 4
        lanes = []

        def lane_alloc(L, li):
            L.tmA = ar.alloc("tmA%d" % li, [512], F32)
            L.tmB = ar.alloc("tmB%d" % li, [768], F32)
            L.tmbf = ar.alloc("tmbf%d" % li, [768], BF16)
            L.ssb = ar.alloc("ssb%d" % li, [16], F32)
            L.scr = [ar.alloc("scr%d_%d" % (li, i), [256], F32) for i in range(4)]
            L.oring = [ar.alloc("oring%d_%d" % (li, i), [576], F32) for i in range(1)]
            L.oi = 0
            L.live = True

        def lane_free(L):
            for b_ in [L.tmA, L.tmB, L.tmbf, L.ssb] + L.scr + L.oring:
                b_.free()
            L.live = False
        for li in range(NLANE):
            L = _Lane()
            L.live = False
            if li < 2:
                lane_alloc(L, li)
            lanes.append(L)

        class _Cur:
            def __getattr__(self, k):
                return getattr(lanes[self.__dict__["li"]], k)

            def out_stage(self):
                L = lanes[self.__dict__["li"]]
                b = L.oring[0]
                L.oi += 1
                return b
        S = _Cur()
        S.__dict__["li"] = 0
        ucnt = {"i": 0}

        def alltk(buf, n=24):
            return [buf.t(i) for i in range(n)]

        def uniq(name):
            ucnt["i"] += 1
            return "%s_%d" % (name, ucnt["i"])

        def run_lanes(gens, width=3, background=None):
            gens = list(gens)
            active = {}
            nxt = 0
            dyn = []
            for li in range(2, width):
                if len(gens) > li and not lanes[li].live:
                    lane_alloc(lanes[li], li)
                    dyn.append(lanes[li])
            width = sum(1 for li in range(width) if lanes[li].live)
            while nxt < len(gens) or active:
                if background:
                    pump()
                for li in range(width):
                    if li not in active and nxt < len(gens):
                        active[li] = gens[nxt]
                        nxt += 1
                for li in list(active.keys()):
                    S.__dict__["li"] = li
                    bstate["lanes"] = True
                    bstate["li"] = li
                    try:
                        next(active[li])
                    except StopIteration:
                        del active[li]
                    bstate["lanes"] = False
            for L in dyn:
                lane_free(L)
            S.__dict__["li"] = 0

        def drain(gen):
            S.__dict__["li"] = 0
            for _ in gen:
                pass

        xTp = ar.alloc("xTp", [8, TP], F32)
        xTs = ar.alloc("xTs", [8, TS], F32)

        def evac_copy(out, in_, reads, writes, force_dve=False):
            fw.flip ^= 1
            if fw.flip and not force_dve:
                return fw.op(act, A.copy, out, in_, reads=reads, writes=writes)
            return fw.op(dve, V.tensor_copy, out, in_, reads=reads, writes=writes)

        def load_xT(d_x, T, dst, dst_off=0, q=sp):
            ring = [ar.alloc(uniq("xld"), [1024], F32) for i in range(2)]
            for it, (t0, n) in enumerate(ttiles(T)):
                xb = ring[it % 2]
                fw.dma(q, xb.ap[:n, :], d_x[t0:t0 + n, :], writes=[xb.t()])
                yield
                for g in range(2):
                    bk, bt = bank("s")
                    for c4 in range(4):
                        c = g * 4 + c4
                        fw.op(pe, nc.tensor.transpose, bk[:, c4 * 128:c4 * 128 + n], xb.ap[:n, c * 128:(c + 1) * 128],
                              ident.ap[:n, :n], reads=[xb.t(), ident.t()], writes=[bt])
                        yield
                    src = bk[:, :].rearrange("p (a b) -> p a b", a=4)[:, :, :n]
                    evac_copy(dst.ap[:, g * 4:g * 4 + 4, dst_off + t0:dst_off + t0 + n], src, [bt], [dst.t()])
                    yield
            for b in ring:
                b.free()

        run_lanes([load_xT(d_xp, TP, xTp), load_xT(d_xs, TS, xTs)])

        siluc = ar.alloc("siluc", [8, 2], BF16)
        fw.op(act, A.activation, siluc.ap, cst("condT").rearrange("p (c j) -> p c j", j=2), AF.Silu,
              reads=[consts.t()], writes=[siluc.t()])
        mod = [ar.alloc("mod%d" % l, [48, 2], F32) for l in range(2)]
        gm = [[ar.alloc("gm%d_%d" % (l, k), [8, 2], F32) for k in range(2)] for l in range(2)]

        wmring = [ar.alloc("wmod%d" % i, [8, 512], BF16) for i in range(2)]
        wmstate = {"i": 0}

        def mod_gen(l, blks, spread=0):
            for blk in blks:
                wb = wmring[wmstate["i"] % 2]
                wmstate["i"] += 1
                fw.dma(pool, wb.ap, d_wmod[l, blk], writes=[wb.t()])
                yield
                bk, bt = bank("bg")
                for m in range(4):
                    fw.mm_group(bk[:, m * 2:m * 2 + 2],
                                [(wb.ap[:, kc, m * 128:(m + 1) * 128], siluc.ap[:, kc, :]) for kc in range(8)],
                                reads=[wb.t(), siluc.t()], writes=[bt])
                    yield
                v = blk // 2
                fw.op(dve, V.tensor_tensor, mod[l].ap[:, blk * 4:blk * 4 + 4, :], bk[:, 0:8].rearrange("p (c j) -> p c j", j=2),
                      cst("bmod%d" % l, blk * 4, blk * 4 + 4).unsqueeze(2).broadcast_to([128, 4, 2]), ALU.add,
                      reads=[bt, consts.t()], writes=[mod[l].t(v)])
                yield
                if blk in (3, 9):
                    k = 0 if blk == 3 else 1
                    gname, scoff = (("gmix%d" % l, 8), ("gffn%d" % l, 32))[k]
                    fw.op(dve, V.tensor_scalar_add, gm[l][k].ap, mod[l].ap[:, scoff:scoff + 8, :], 1.0,
                          reads=[mod[l].t(v)], writes=[gm[l][k].t()])
                    fw.op(dve, V.tensor_tensor, gm[l][k].ap, gm[l][k].ap,
                          cst(gname).unsqueeze(2).broadcast_to([128, 8, 2]), ALU.mult,
                          reads=[gm[l][k].t(), consts.t()], writes=[gm[l][k].t()])
                    yield
                bgstate["done"] += 1
                for _ in range(spread):
                    yield

        drain(mod_gen(0, [0, 1, 2, 3]))
        bgstate["done"] = 0

        def mod_rest():
            yield from mod_gen(0, [4, 5, 6, 7, 8, 9, 10, 11], spread=MOD_SPREAD)
            yield from mod_gen(1, list(range(12)), spread=MOD_SPREAD)
        bgstate["gen"] = mod_rest()

        def pump(k=1):
            for _ in range(k):
                if bgstate["gen"] is None:
                    return
                try:
                    next(bgstate["gen"])
                except StopIteration:
                    bgstate["gen"] = None

        def ensure_mod(nblocks):
            while bgstate["gen"] is not None and bgstate["done"] < nblocks:
                pump()

        def mod_sh(l, k, c, j):
            o = 0 if k == 0 else 24
            return mod[l].ap[:, o + c, j:j + 1]

        def mod_gate(l, k, c, j):
            o = 16 if k == 0 else 40
            return mod[l].ap[:, o + c, j:j + 1]

        def norm_mod(xsrc, xt, T, l, k, j, hdst, hdst_t, mask=None, plain_gain=None, out_f32=None):
            sq = ar.alloc(uniq("nm_sq"), [8, T], BF16)
            rs = ar.alloc(uniq("nm_rs"), [T], F32)
            tmp = [ar.alloc(uniq("nm_tmp"), [T], F32) for i in range(2)]
            for g in range(2):
                fw.op(act, A.activation, sq.ap[:, g * 4:g * 4 + 4, :], xsrc[:, g * 4:g * 4 + 4, :], AF.Square,
                      reads=[xt], writes=[sq.t(g)])
                yield
            for (a, b) in nchunks(T):
                bk, bt = bank("s")
                fw.mm_group(bk[:, 0:b - a], [(onesb.ap, sq.ap[:, c, a:b]) for c in range(8)],
                            reads=[onesb.t(), sq.t(0), sq.t(1)], writes=[bt])
                yield
                fw.op(act, A.activation, rs.ap[:, a:b], bk[:, 0:b - a], AF.Ln, bias=epsb.ap[:, 0:1], scale=1.0 / 1024,
                      reads=[bt, epsb.t()], writes=[rs.t()])
                yield
            fw.op(act, A.activation, rs.ap, rs.ap, AF.Exp, scale=-0.5, reads=[rs.t()], writes=[rs.t()])
            yield
            for c in range(8):
                tb = tmp[c % 2]
                fw.op(dve, V.tensor_tensor, tb.ap, xsrc[:, c, :], rs.ap, ALU.mult,
                      reads=[xt, rs.t()], writes=[tb.t()])
                yield
                if plain_gain is not None:
                    fw.op(act, A.activation, hdst[:, c, :], tb.ap, AF.Identity, scale=plain_gain[:, c:c + 1],
                          reads=[tb.t(), consts.t()], writes=[hdst_t])
                    yield
                else:
                    fw.op(act, A.activation, hdst[:, c, :], tb.ap, AF.Identity,
                          bias=mod_sh(l, k, c, j), scale=gm[l][k].ap[:, c, j:j + 1],
                          reads=[tb.t(), mod[l].t(0 if k == 0 else 3), gm[l][k].t()], writes=[hdst_t])
                    yield
                    if mask is not None:
                        fw.op(dve, V.tensor_tensor, hdst[:, c, :], hdst[:, c, :], mask, ALU.mult,
                              reads=[hdst_t, vmbox["v"].t()], writes=[hdst_t])
                        yield
            sq.free()
            rs.free()
            for b in tmp:
                b.free()

        vmbox = {}

        def load_w(name, d_w, kchunks, ncols, rows_per_chunk=128):
            wb = ar.alloc(name, [kchunks, ncols], BF16)
            for kc in range(kchunks):
                fw.dma(pool, wb.ap[:rows_per_chunk, kc, :], d_w[kc * rows_per_chunk:(kc + 1) * rows_per_chunk, :],
                       writes=[wb.t()])
            return wb

        def rstd_from_ss(ss_ap, n, width, inv_n, ss_t):
            fw.op(act, A.activation, ss_ap, ss_ap, AF.Ln, bias=epsb.ap[:n, 0:1], scale=inv_n, reads=[ss_t, epsb.t()], writes=[ss_t])
            fw.op(act, A.activation, ss_ap, ss_ap, AF.Exp, scale=-0.5, reads=[ss_t], writes=[ss_t])

        def rope_tm(x_ap, x_t, n, H, D, tab_ap, tab_t, scr_, view3=None, dst3=None, dst_t=None, on_dve=False):
            h2 = D // 2
            if view3 is None:
                view3 = x_ap.rearrange("p (h d) -> p h d", h=H)
            xv = view3.rearrange("p h (i two) -> p h i two", two=2)
            x1 = xv[:, :, :, 0]
            x2 = xv[:, :, :, 1]
            if dst3 is None:
                o1, o2, o_t = x1, x2, x_t
            else:
                ov = dst3.rearrange("p h (i two) -> p h i two", two=2)
                o1, o2, o_t = ov[:, :, :, 0], ov[:, :, :, 1], dst_t
            cs = tab_ap[:, 0:h2].unsqueeze(1).broadcast_to([n, H, h2])
            sn = tab_ap[:, h2:D].unsqueeze(1).broadcast_to([n, H, h2])
            s = [scr_[i].ap[:n, 0:H * h2].rearrange("p (h i) -> p h i", h=H) for i in range(4)]
            ts = [scr_[i].t() for i in range(4)]
            e_, E_ = (dve, V) if on_dve else (pool, G)
            fw.op(e_, E_.tensor_tensor, s[0], x1, cs, ALU.mult, reads=[x_t, tab_t], writes=[ts[0]])
            yield
            fw.op(e_, E_.tensor_tensor, s[1], x2, sn, ALU.mult, reads=[x_t, tab_t], writes=[ts[1]])
            yield
            fw.op(e_, E_.tensor_tensor, s[2], x1, sn, ALU.mult, reads=[x_t, tab_t], writes=[ts[2]])
            yield
            fw.op(e_, E_.tensor_tensor, s[3], x2, cs, ALU.mult, reads=[x_t, tab_t], writes=[ts[3]])
            yield
            fw.op(e_, E_.tensor_tensor, o1, s[0], s[1], ALU.subtract, reads=[ts[0], ts[1]], writes=[o_t])
            yield
            fw.op(e_, E_.tensor_tensor, o2, s[2], s[3], ALU.add, reads=[ts[2], ts[3]], writes=[o_t])
            yield

        def transpose_to(dst_ap_fn, dst_t, src_bf, src_t, n, nblk, blkw=128, group_fn=None):
            done = 0
            while done < nblk:
                g = min(4, nblk - done)
                bk, bt = bank("s")
                bkb = bk[:, :].bitcast(BF16)
                for i in range(g):
                    c = done + i
                    fw.op(pe, nc.tensor.transpose, bkb[:blkw, i * 128:i * 128 + n], src_bf[:n, c * blkw:(c + 1) * blkw],
                          identb.ap[:n, :n], reads=[src_t, identb.t()], writes=[bt])
                    yield
                if group_fn is not None:
                    for (dap, r0_, r1_) in group_fn(done, g):
                        srcv = bkb[r0_:r1_, 0:g * 128].rearrange("p (c m) -> p c m", c=g)[:, :, 0:n]
                        evac_copy(dap, srcv, [bt], [dst_t])
                        yield
                    done += g
                    continue
                for i in range(g):
                    c = done + i
                    d = dst_ap_fn(c)
                    if isinstance(d, list):
                        for (dap, r0_, r1_) in d:
                            evac_copy(dap, bkb[r0_:r1_, i * 128:i * 128 + n], [bt], [dst_t])
                            yield
                    else:
                        evac_copy(d, bkb[:blkw, i * 128:i * 128 + n], [bt], [dst_t])
                        yield
                done += g

        ropeA_ext = ar.alloc("ropeA_ext", [7, 64], F32)
        ropeB_ext = ar.alloc("ropeB_ext", [7, 32], F32)
        fw.dma(sp, ropeA_ext.ap, d_ropeA_ext, writes=[ropeA_ext.t()])
        fw.dma(sp, ropeB_ext.ap, d_ropeB_ext, writes=[ropeB_ext.t()])

        PTN = 6
        BURST = 2
        ptring = []
        ptstate = {"i": 0}
        rlring = []
        rlstate = {"i": 0}
        esink = ar.alloc("esink", [16], F32)
        fw.op(act, A.activation, esink.ap, cst("sink"), AF.Exp, reads=[consts.t()], writes=[esink.t()])

        att_queue = []
        att_pres = []

        def att_pre(fn, at_burst=0):
            att_pres.append((at_burst, fn))

        def attend(nq, s_pairs_fn, key_tiles, v_fn, scale, dst_ap, dst_t, reads_q, mask_fn=None, sink_col=None, act_recip=True):
            pres = {}
            for (bi, fn) in att_pres:
                pres.setdefault(bi, []).append(fn)
            del att_pres[:]
            att_queue.append(dict(nq=nq, s_pairs_fn=s_pairs_fn, key_tiles=key_tiles, v_fn=v_fn, scale=scale, dst_ap=dst_ap,
                                  dst_t=dst_t, reads_q=reads_q, mask_fn=mask_fn, sink_col=sink_col, act_recip=act_recip,
                                  pres=pres, acc=None))

        def _att_s(p, bi):
            nq = p["nq"]
            groups = []
            items = []
            for ki in p["bursts"][bi]:
                nk = p["key_tiles"][ki]
                sb_, sbt = bank("s")
                pairs, rtr = p["s_pairs_fn"](ki)
                np_ = len(pairs)
                groups.append((sb_[:nk, 0:nq], [(l, r, i == 0, i == np_ - 1) for i, (l, r) in enumerate(pairs)],
                               rtr + p["reads_q"], [sbt]))
                items.append((ki, nk, sb_, sbt))
            fw.mm_burst(groups)
            cur = []
            for (ki, nk, sb_, sbt) in items:
                pt = ptring[ptstate["i"] % PTN]
                ptstate["i"] += 1
                fw.op(act, A.activation, pt.ap[:nk, 0:nq], sb_[:nk, 0:nq], AF.Exp, scale=p["scale"],
                      reads=[sbt], writes=[pt.t()])
                if p["mask_fn"] is not None:
                    m = p["mask_fn"](ki)
                    if m is not None:
                        map_, mt = m
                        fw.op(dve, V.tensor_tensor, pt.ap[:nk, 0:nq], pt.ap[:nk, 0:nq], map_, ALU.mult,
                              reads=[pt.t(), mt], writes=[pt.t()])
                cur.append((ki, nk, pt))
            return cur

        def _att_pv(p, bi, cur):
            nq = p["nq"]
            nkt = len(p["key_tiles"])
            if p["acc"] is None:
                p["acc"] = bank("a")
            acc, acct = p["acc"]
            groups = []
            for (pki, pnk, ppt) in cur:
                vap, vtr = p["v_fn"](pki)
                groups.append((acc[:, 0:nq], [(vap, ppt.ap[:pnk, 0:nq], pki == 0, pki == nkt - 1)],
                               vtr + [ppt.t()], [acct]))
            fw.mm_burst(groups)
            if bi == len(p["bursts"]) - 1:
                _att_fin(p)

        def flush_attends():
            plans = att_queue[:]
            del att_queue[:]
            for i in range(PTN):
                ptring.append(ar.alloc(uniq("pt"), [512], BF16))
            rlring.append(ar.alloc(uniq("rlb"), [512], F32))
            LAG = 2
            flat = []
            for p in plans:
                nkt = len(p["key_tiles"])
                p["bursts"] = [list(range(i, min(nkt, i + BURST))) for i in range(0, nkt, BURST)]
                nb = len(p["bursts"])
                pres = {}
                for k_, fns in p["pres"].items():
                    pres.setdefault(min(k_, nb - 1), []).extend(fns)
                p["pres"] = pres
                for bi in range(nb):
                    flat.append((p, bi))
            pendq = []
            for (p, bi) in flat:
                pump()
                for fn in p["pres"].get(bi, []):
                    fn()
                cur = _att_s(p, bi)
                pendq.append((p, bi, cur))
                if len(pendq) > LAG:
                    _att_pv(*pendq.pop(0))
            while pendq:
                _att_pv(*pendq.pop(0))
            for b_ in ptring + rlring:
                b_.free()
            del ptring[:]
            del rlring[:]

        def _att_fin(p):
            nq, sink_col, act_recip = p["nq"], p["sink_col"], p["act_recip"]
            dst_ap, dst_t = p["dst_ap"], p["dst_t"]
            acc, acct = p["acc"]
            rlb = rlring[0]
            rlstate["i"] += 1
            if act_recip:
                if sink_col is not None:
                    fw.op(act, A.activation, rlb.ap[64:128, 0:nq], acc[64:128, 0:nq], AF.Ln,
                          bias=esink.ap[64:128, sink_col:sink_col + 1], reads=[acct, esink.t()], writes=[rlb.t()])
                else:
                    fw.op(act, A.activation, rlb.ap[64:128, 0:nq], acc[64:128, 0:nq], AF.Ln, reads=[acct], writes=[rlb.t()])
                fw.op(act, A.activation, rlb.ap[64:128, 0:nq], rlb.ap[64:128, 0:nq], AF.Exp, scale=-1.0,
                      reads=[rlb.t()], writes=[rlb.t()])
            elif sink_col is not None:
                fw.op(dve, V.tensor_scalar_add, rlb.ap[64:128, 0:nq], acc[64:128, 0:nq], esink.ap[64:128, sink_col:sink_col + 1],
                      reads=[acct, esink.t()], writes=[rlb.t()])
                fw.op(dve, V.reciprocal, rlb.ap[64:128, 0:nq], rlb.ap[64:128, 0:nq], reads=[rlb.t()], writes=[rlb.t()])
            else:
                fw.op(dve, V.reciprocal, rlb.ap[64:128, 0:nq], acc[64:128, 0:nq], reads=[acct], writes=[rlb.t()])
            fw.op(dve, V.tensor_tensor, dst_ap, acc[0:64, 0:nq], rlb.ap[64:128, 0:nq], ALU.mult,
                  reads=[acct, rlb.t()], writes=[dst_t])

        ffn_pref = {}

        def ffn_prefetch(l):
            wring = [ar.alloc("wup%d" % i, [8, 256], BF16) for i in range(3)]
            for jj in range(2):
                fw.dma(pool, wring[jj % 3].ap, d_wup[l, jj], writes=[wring[jj % 3].t()])
            ffn_pref["wring"] = wring

        def ffn(l, groups):
            Ttot = sum(g["T"] for g in groups)
            actT = ar.alloc("actT", [NJ, Ttot], BF16)
            ctmp = [ar.alloc("ctmp%d" % i, [512], F32) for i in range(2)]
            cst_i = {"i": 0}
            wring = ffn_pref.pop("wring")
            cw = cst("convw%d" % l).rearrange("p (j k) -> p j k", k=3)
            cb = cst("convb%d" % l)
            work = []
            toff = 0
            for g in groups:
                g["toff"] = toff
                for (sa, sb2) in g["segs"]:
                    for (a, b) in nchunks(sb2 - sa, 510):
                        work.append((g, sa + a, sa + b, sa, sb2))
                toff += g["T"]

            def issue_w(jj):
                wb = wring[jj % 3]
                fw.dma(pool, wb.ap, d_wup[l, jj], writes=[wb.t()])

            dring = [ar.alloc("wdn%d" % i, [NJ, 128], BF16) for i in range(2)]

            def issue_d(m):
                fw.dma(pool, dring[m % 2].ap, d_wdn[l, m], writes=[dring[m % 2].t()])

            for jj in range(NJ):
                if jj == NJ - 4:
                    issue_d(0)
                if jj + 2 < NJ:
                    issue_w(jj + 2)
                wb = wring[jj % 3]
                for (g, a, b, sa, sb2) in work:
                    ga = max(a - 1, sa)
                    gb = min(b + 1, sb2)
                    ng = gb - ga
                    n = b - a
                    gk, gt = bank("s")
                    vk, vt = bank("s")
                    fw.mm_group(gk[:, 0:ng], [(wb.ap[:, kc, 0:128], g["hT"][:, kc, ga:gb]) for kc in range(8)],
                                reads=[wb.t(), g["ht"]], writes=[gt])
                    fw.mm_group(vk[:, 0:n], [(wb.ap[:, kc, 128:256], g["hT"][:, kc, a:b]) for kc in range(8)],
                                reads=[wb.t(), g["ht"]], writes=[vt])
                    ct = ctmp[cst_i["i"] % 2]
                    cst_i["i"] += 1
                    o = a - ga
                    fw.op(act, A.activation, ct.ap[:, 0:n], gk[:, o:o + n], AF.Identity, bias=cb[:, jj:jj + 1],
                          scale=cw[:, jj, 1:2], reads=[gt, consts.t()], writes=[ct.t()])
                    lo = 0 if o == 1 else 1
                    if n - lo > 0:
                        fw.op(dve, V.scalar_tensor_tensor, ct.ap[:, lo:n], gk[:, o + lo - 1:o + n - 1], cw[:, jj, 0:1],
                              ct.ap[:, lo:n], ALU.mult, ALU.add, reads=[gt, ct.t(), consts.t()], writes=[ct.t()])
                    hi = n if gb > b else n - 1
                    if hi > 0:
                        fw.op(dve, V.scalar_tensor_tensor, ct.ap[:, 0:hi], gk[:, o + 1:o + 1 + hi], cw[:, jj, 2:3],
                              ct.ap[:, 0:hi], ALU.mult, ALU.add, reads=[gt, ct.t(), consts.t()], writes=[ct.t()])
                    fw.op(act, A.activation, ct.ap[:, 0:n], ct.ap[:, 0:n], AF.Silu, reads=[ct.t()], writes=[ct.t()])
                    fw.op(dve, V.tensor_tensor, actT.ap[:, jj, g["toff"] + a:g["toff"] + b], ct.ap[:, 0:n], vk[:, 0:n],
                          ALU.mult, reads=[ct.t(), vt], writes=[actT.t(jj)])
            for b_ in wring:
                b_.free()
            for b_ in ctmp:
                b_.free()
            allact = [actT.t(jj) for jj in range(NJ)]
            for m in range(8):
                if m + 1 < 8:
                    issue_d(m + 1)
                wd = dring[m % 2]
                for g in groups:
                    for (a, b) in nchunks(g["T"]):
                        bk, bt = bank("s")
                        fw.mm_group(bk[:, 0:b - a],
                                    [(wd.ap[:, jj, :], actT.ap[:, jj, g["toff"] + a:g["toff"] + b]) for jj in range(NJ)],
                                    reads=[wd.t()] + allact, writes=[bt])
                        fw.op(dve, V.scalar_tensor_tensor, g["xT"][:, m, a:b], bk[:, 0:b - a], mod_gate(l, 1, m, g["j"]),
                              g["xT"][:, m, a:b], ALU.mult, ALU.add, reads=[bt, mod[l].t(5), g["xt"]], writes=[g["xt"]])
            for b_ in dring:
                b_.free()
            actT.free()

        def out_proj(l, wout, OT, groups):
            allo = [OT.t(c) for c in range(8)]
            for m in range(8):
                for g in groups:
                    for (a, b) in nchunks(g["T"]):
                        bk, bt = bank("s")
                        fw.mm_group(bk[:, 0:b - a],
                                    [(wout.ap[:, kc, m * 128:(m + 1) * 128], OT.ap[:, kc, g["ooff"] + a:g["ooff"] + b])
                                     for kc in range(8)],
                                    reads=[wout.t()] + allo, writes=[bt])
                        fw.op(dve, V.scalar_tensor_tensor, g["xT"][:, m, a:b], bk[:, 0:b - a], mod_gate(l, 0, m, g["j"]),
                              g["xT"][:, m, a:b], ALU.mult, ALU.add, reads=[bt, mod[l].t(2), g["xt"]], writes=[g["xt"]])

        hTp = ar.alloc("hTp", [8, TP], BF16)
        hTs = ar.alloc("hTs", [8, TS], BF16)
        run_lanes([norm_mod(xTp.ap, xTp.t(), TP, 0, 0, 0, hTp.ap, hTp.t()),
                   norm_mod(xTs.ap, xTs.t(), TS, 0, 0, 1, hTs.ap, hTs.t())])

        wine = load_w("wine", d_wine, 8, 1440)
        wuq = load_w("wuq", d_wuq, 3, 768)
        wukv = load_w("wukv", d_wukv, 2, 1024)

        def proj_tm(hT_ap, h_t, t0, n, w, c0, c1):
            bk, bt = bank("s")
            fw.mm_group(bk[:n, 0:c1 - c0], [(hT_ap[:, kc, t0:t0 + n], w.ap[:, kc, c0:c1]) for kc in range(8)],
                        reads=[h_t, w.t()], writes=[bt])
            return bk, bt

        def head_norm(bk, bt, n, H, gname, dst_f32_ap, dst_t):
            w = H * 64
            fw.op(act, A.activation, S.tmA.ap[:n, 0:w], bk[:n, 0:w], AF.Square, reads=[bt], writes=[S.tmA.t()])
            yield
            fw.op(dve, V.tensor_reduce, S.ssb.ap[:n, 0:H], S.tmA.ap[:n, 0:w].rearrange("p (h d) -> p h d", h=H), AX.X, ALU.add,
                  reads=[S.tmA.t()], writes=[S.ssb.t()])
            yield
            rstd_from_ss(S.ssb.ap[:n, 0:H], n, H, 1.0 / 64, S.ssb.t())
            yield
            fw.op(dve, V.tensor_tensor, S.tmA.ap[:n, 0:w].rearrange("p (h d) -> p h d", h=H),
                  bk[:n, 0:w].rearrange("p (h d) -> p h d", h=H),
                  S.ssb.ap[:n, 0:H].unsqueeze(2).broadcast_to([n, H, 64]), ALU.mult,
                  reads=[bt, S.ssb.t()], writes=[S.tmA.t()])
            yield
            fw.op(dve, V.tensor_tensor, dst_f32_ap.rearrange("p (h d) -> p h d", h=H),
                  S.tmA.ap[:n, 0:w].rearrange("p (h d) -> p h d", h=H),
                  cst(gname)[:n, :].unsqueeze(1).broadcast_to([n, H, 64]), ALU.mult,
                  reads=[S.tmA.t(), consts.t()], writes=[dst_t])
            yield

        def row_norm(bk, bt, n, c0, width, gname, dst_ap, dst_t):
            fw.op(dve, V.memset, S.ssb.ap[:n, 8:9], 0.0, writes=[S.ssb.t()])
            yield
            fw.op(act, A.activation, S.tmA.ap[:n, 0:width], bk[:n, c0:c0 + width], AF.Square, accum_out=S.ssb.ap[:n, 8:9],
                  reads=[bt], writes=[S.tmA.t(), S.ssb.t()])
            yield
            rstd_from_ss(S.ssb.ap[:n, 8:9], n, 1, 1.0 / width, S.ssb.t())
            yield
            fw.op(dve, V.scalar_tensor_tensor, dst_ap, bk[:n, c0:c0 + width], S.ssb.ap[:n, 8:9], cst(gname)[:n, :],
                  ALU.mult, ALU.mult, reads=[bt, S.ssb.t(), consts.t()], writes=[dst_t])
            yield

        def even_qa(hT_ap, h_t, t0, n, QaT, tok_off, ropeA=None):
            bk, bt = proj_tm(hT_ap, h_t, t0, n, wine, 0, 512)
            yield
            yield from head_norm(bk, bt, n, 8, "gq", S.tmB.ap[:n, 0:512], S.tmB.t())
            if ropeA is not None:
                yield from rope_tm(S.tmB.ap[:n, 0:512], S.tmB.t(), n, 8, 64, ropeA[0], ropeA[1], S.scr,
                                   dst3=S.tmbf.ap[:n, 0:512].rearrange("p (h d) -> p h d", h=8), dst_t=S.tmbf.t())
            else:
                fw.op(act, A.copy, S.tmbf.ap[:n, 0:512], S.tmB.ap[:n, 0:512], reads=[S.tmB.t()], writes=[S.tmbf.t()])
                yield
            yield from transpose_to(None, QaT.t(tok_off // 128), S.tmbf.ap, S.tmbf.t(), n, 4,
                                    group_fn=lambda c0, g: [(QaT.ap[0:64, 2 * c0:2 * (c0 + g):2, tok_off:tok_off + n], 0, 64),
                                                            (QaT.ap[64:128, 2 * c0 + 1:2 * (c0 + g):2, tok_off:tok_off + n], 64, 128)])

        def even_cq(hT_ap, h_t, t0, n, cqnT, tok_off):
            bk, bt = proj_tm(hT_ap, h_t, t0, n, wine, 768, 1152)
            yield
            yield from row_norm(bk, bt, n, 0, 384, "gcq", S.tmbf.ap[:n, 0:384], S.tmbf.t())
            yield from transpose_to(None, cqnT.t(tok_off // 128), S.tmbf.ap, S.tmbf.t(), n, 3,
                                    group_fn=lambda c0, g: [(cqnT.ap[:, c0:c0 + g, tok_off:tok_off + n], 0, 128)])

        def even_qb(cqnT, t0, n, QbT, ropeB=None):
            for half in range(2):
                bk, bt = bank("s")
                fw.mm_group(bk[:n, 0:384], [(cqnT.ap[:, kc, t0:t0 + n], wuq.ap[:, kc, half * 384:(half + 1) * 384]) for kc in range(3)],
                            reads=[cqnT.t(t0 // 128), wuq.t()], writes=[bt])
                yield
                if ropeB is not None:
                    fw.op(act, A.copy, S.tmB.ap[:n, half * 384:(half + 1) * 384], bk[:n, 0:384], reads=[bt], writes=[S.tmB.t()])
                    yield
                else:
                    fw.op(act, A.copy, S.tmbf.ap[:n, half * 384:(half + 1) * 384], bk[:n, 0:384], reads=[bt], writes=[S.tmbf.t()])
                    yield
            if ropeB is not None:
                v3 = S.tmB.ap[:n, 0:768].rearrange("p (h d) -> p h d", h=8)[:, :, 64:96]
                yield from rope_tm(None, S.tmB.t(), n, 8, 32, ropeB[0], ropeB[1], S.scr, view3=v3)
                fw.op(act, A.copy, S.tmbf.ap[:n, 0:768], S.tmB.ap[:n, 0:768], reads=[S.tmB.t()], writes=[S.tmbf.t()])
                yield
            yield from transpose_to(None, QbT.t(t0 // 128), S.tmbf.ap, S.tmbf.t(), n, 8, blkw=96,
                                    group_fn=lambda c0, g: [(QbT.ap[0:96, c0:c0 + g, t0:t0 + n], 0, 96)])

        def even_kv_side(src, n, KaT2, Va, ckvnT, kpeT2, key_off, vtile, outs=None, ropeA=None, ropeB=None):
            if src[0] == "proj":
                _, hT_ap, h_t, t0, wkv, ckv0, cc0 = src
                bk, bt = proj_tm(hT_ap, h_t, t0, n, wkv, ckv0, ckv0 + 256)
                yield
                kf = S.out_stage()
                yield from head_norm(bk, bt, n, 2, "gk", kf.ap[:n, 0:128], kf.t())
                fw.op(act, A.copy, kf.ap[:n, 128:256], bk[:n, 128:256], reads=[bt], writes=[kf.t()])
                yield
                bk2, bt2 = proj_tm(hT_ap, h_t, t0, n, wkv, cc0, cc0 + 288)
                yield
                yield from row_norm(bk2, bt2, n, 0, 256, "gckv", kf.ap[:n, 256:512], kf.t())
                fw.op(act, A.copy, kf.ap[:n, 512:544], bk2[:n, 256:288], reads=[bt2], writes=[kf.t()])
                yield
                if outs is not None:
                    r0 = outs
                    fw.dma(sp, o_ak[r0:r0 + n, :], kf.ap[:n, 0:128], reads=[kf.t()], is_output=True)
                    yield
                    fw.dma(sp, o_av[r0:r0 + n, :], kf.ap[:n, 128:256], reads=[kf.t()], is_output=True)
                    yield
                    fw.dma(sp, o_ckv[r0:r0 + n, :], kf.ap[:n, 256:512], reads=[kf.t()], is_output=True)
                    yield
                    fw.dma(sp, o_kpe[r0:r0 + n, :], kf.ap[:n, 512:544], reads=[kf.t()], is_output=True)
                    yield
                kfa, kft = kf.ap, kf.t()
                if ropeA is not None:
                    yield from rope_tm(kfa[:n, 0:128], kft, n, 2, 64, ropeA[0], ropeA[1], S.scr)
                    yield from rope_tm(kfa[:n, 512:544], kft, n, 1, 32, ropeB[0], ropeB[1], S.scr)
            else:
                _, kf = src
                kfa, kft = kf.ap, kf.t()
            fw.op(dve, V.tensor_copy, S.tmbf.ap[:n, 0:256].rearrange("p (k r d) -> p k r d", k=2, r=2),
                  kfa[:n, 0:128].rearrange("p (k d) -> p k d", k=2).unsqueeze(2).broadcast_to([n, 2, 2, 64]),
                  reads=[kft], writes=[S.tmbf.t()])
            yield
            yield from transpose_to(None, KaT2.t(key_off // 128), S.tmbf.ap, S.tmbf.t(), n, 2,
                                    group_fn=lambda c0, g: [(KaT2.ap[:, c0:c0 + g, key_off:key_off + n], 0, 128)])
            fw.op(act, A.copy, Va.ap[:n, vtile, :, 0:64], kfa[:n, 128:256].rearrange("p (k d) -> p k d", k=2),
                  reads=[kft], writes=[Va.t(vtile)])
            yield
            fw.op(dve, V.tensor_copy, S.tmbf.ap[:n, 256:512], kfa[:n, 256:512], reads=[kft], writes=[S.tmbf.t()])
            yield
            yield from transpose_to(None, ckvnT.t(key_off // 128), S.tmbf.ap[:, 256:512], S.tmbf.t(), n, 2,
                                    group_fn=lambda c0, g: [(ckvnT.ap[:, c0:c0 + g, key_off:key_off + n], 0, 128)])
            fw.op(dve, V.tensor_copy, S.tmbf.ap[:n, 576:608], kfa[:n, 512:544], reads=[kft], writes=[S.tmbf.t()])
            yield
            yield from transpose_to(lambda c: [(kpeT2.ap[64:96, key_off:key_off + n], 64, 96)], kpeT2.t(key_off // 128),
                         S.tmbf.ap[:, 512:608], S.tmbf.t(), n, 1, blkw=96)

        def ones_cols(Vbuf, ntile, nh):
            fw.op(dve, V.memset, Vbuf.ap[:, :, :, 64:128], 1.0, writes=alltk(Vbuf, ntile))

        def even_attention(T, QaT, QbT, KaT2, Va, ckvnT, kpeT2, NK, OT, ooff, qsegs, act_recip=True, after_A=None):
            SC_A = 64 ** -0.5
            SC_B = 96 ** -0.5
            for h in range(8):
                kvh = h // 4
                pb = (h % 2) * 64
                for (q0, q1, k0, k1) in qsegs:
                    kts = ttiles(k1 - k0)
                    for (a, b) in nchunks(q1 - q0):
                        qa, qb = q0 + a, q0 + b
                        attend(qb - qa,
                               lambda ki, kts=kts, k0=k0, kvh=kvh, h=h, qa=qa, qb=qb: (
                                   [(KaT2.ap[:, kvh, k0 + kts[ki][0]:k0 + kts[ki][0] + kts[ki][1]],
                                     QaT.ap[:, h, qa:qb])], [KaT2.t((k0 + kts[ki][0]) // 128)]),
                               [kt[1] for kt in kts],
                               lambda ki, kts=kts, k0=k0, kvh=kvh: (
                                   Va.ap[:kts[ki][1], (k0 + kts[ki][0]) // 128, kvh, :], [Va.t((k0 + kts[ki][0]) // 128)]),
                               SC_A, OT.ap[pb:pb + 64, h // 2, ooff + qa:ooff + qb], OT.t(h // 2), alltk(QaT, 8), act_recip=act_recip)
            flush_attends()
            if after_A is not None:
                after_A()
            KbT = [ar.alloc("KbT%d" % i, [NK], BF16) for i in range(2)]
            nvt = cdiv(NK, 128)
            Vb = [ar.alloc("Vb%d" % i, [nvt, 2, 128], BF16) for i in range(2)]
            for i in range(2):
                ones_cols(Vb[i], nvt, 2)
                fw.op(dve, V.memset, KbT[i].ap[96:128, :], 0.0, writes=[KbT[i].t()])
            def build_V(c):
                vb = Vb[c % 2]
                for vt_, (k0_, nk_) in enumerate(ttiles(NK)):
                    bk, bt = bank("s")
                    fw.mm_group(bk[:nk_, 0:128],
                                [(ckvnT.ap[:, kc, k0_:k0_ + nk_], wukv.ap[:, kc, 512 + c * 128:512 + (c + 1) * 128]) for kc in range(2)],
                                reads=[wukv.t(), ckvnT.t(k0_ // 128)], writes=[bt])
                    evac_copy(vb.ap[:nk_, vt_, :, 0:64], bk[:nk_, 0:128].rearrange("p (k d) -> p k d", k=2), [bt], [vb.t()],
                              force_dve=(NK > 1024))

            def build_K(h):
                kb = KbT[h % 2]
                for (a, b) in nchunks(NK):
                    bk, bt = bank("s")
                    fw.mm_group(bk[0:64, 0:b - a], [(wukv.ap[:, kc, h * 64:(h + 1) * 64], ckvnT.ap[:, kc, a:b]) for kc in range(2)],
                                reads=[wukv.t()] + [ckvnT.t(i_) for i_ in range(a // 128, cdiv(b, 128))], writes=[bt])
                    evac_copy(kb.ap[0:64, a:b], bk[0:64, 0:b - a], [bt], [kb.t()], force_dve=(NK > 1024))
                fw.op(dve, V.tensor_copy, kb.ap[64:96, :], kpeT2.ap[64:96, 0:NK], reads=alltk(kpeT2, cdiv(NK, 128)), writes=[kb.t()])

            build_V(0)
            build_K(0)
            for h in range(8):
                c, hh = h // 2, h % 2
                if h + 1 < 8:
                    att_pre(lambda h=h: build_K(h + 1), 0)
                    if (h + 1) % 2 == 0:
                        att_pre(lambda c=c: build_V(c + 1), 3)
                vb = Vb[c % 2]
                kb = KbT[h % 2]
                pb = hh * 64
                for (q0, q1, k0, k1) in qsegs:
                    kts = ttiles(k1 - k0)
                    for (a, b) in nchunks(q1 - q0):
                        qa, qb = q0 + a, q0 + b
                        attend(qb - qa,
                               lambda ki, kts=kts, k0=k0, h=h, qa=qa, qb=qb, kb=kb: (
                                   [(kb.ap[:, k0 + kts[ki][0]:k0 + kts[ki][0] + kts[ki][1]],
                                     QbT.ap[:, h, qa:qb])], [kb.t()]),
                               [kt[1] for kt in kts],
                               lambda ki, kts=kts, k0=k0, hh=hh, vb=vb: (
                                   vb.ap[:kts[ki][1], (k0 + kts[ki][0]) // 128, hh, :], [vb.t()]),
                               SC_B, OT.ap[pb:pb + 64, 4 + c, ooff + qa:ooff + qb], OT.t(4 + c),
                               alltk(QbT, 8), act_recip=act_recip)
            flush_attends()
            for b_ in KbT + Vb:
                b_.free()

        OT = ar.alloc("OTp", [8, TP], BF16)

        QaT = ar.alloc("QaTp", [8, TP], BF16)
        cqnT = ar.alloc("cqnTp", [3, TP], BF16)
        QbT = ar.alloc("QbTp", [8, TP], BF16)
        fw.op(dve, V.memset, QaT.ap, 0.0, writes=alltk(QaT, 8))
        fw.op(dve, V.memset, QbT.ap[96:128, :, :], 0.0, writes=alltk(QbT, 8))
        KaT2 = ar.alloc("KaT2p", [2, TP], BF16)
        Va = ar.alloc("Vap", [4, 2, 128], BF16)
        ckvnT = ar.alloc("ckvnTp", [2, TP], BF16)
        kpeT2 = ar.alloc("kpeT2p", [TP], BF16)
        ones_cols(Va, 4, 2)
        def chain_p0a(ti, t0, n, QaT=QaT):
            yield from even_qa(hTp.ap, hTp.t(), t0, n, QaT, t0)

        def chain_p0b(ti, t0, n, cqnT=cqnT, QbT=QbT):
            yield from even_cq(hTp.ap, hTp.t(), t0, n, cqnT, t0)
            yield from even_qb(cqnT, t0, n, QbT)

        def chain_p0c(ti, t0, n, KaT2=KaT2, Va=Va, ckvnT=ckvnT, kpeT2=kpeT2):
            yield from even_kv_side(("proj", hTp.ap, hTp.t(), t0, wine, 512, 1152), n, KaT2, Va, ckvnT, kpeT2, t0, ti, outs=t0)
        gl_ = []
        for ti, (t0, n) in enumerate(ttiles(TP)):
            gl_ += [chain_p0b(ti, t0, n), chain_p0a(ti, t0, n), chain_p0c(ti, t0, n)]
        run_lanes(gl_, background=True)
        even_attention(TP, QaT, QbT, KaT2, Va, ckvnT, kpeT2, TP, OT, 0,
                       [(0, SEQ, 0, SEQ), (SEQ, 2 * SEQ, SEQ, 2 * SEQ)])
        for b_ in (QaT, cqnT, QbT, KaT2, Va, ckvnT, kpeT2):
            b_.free()

        hTp.free()
        wukv.free()
        ensure_mod(2)
        gp = dict(T=TP, ooff=0, xT=xTp.ap, xt=xTp.t(), j=0)
        woute = load_w("woute", d_woute, 8, 1024)
        out_proj(0, woute, OT, [gp])
        woute.free()
        OT.free()
        ensure_mod(20)
        for b_ in wmring:
            b_.free()
        if STAGE >= 2:
            QaT = ar.alloc("QaTs", [8, TS], BF16)
            QbT = ar.alloc("QbTs", [8, TS], BF16)
            fw.op(dve, V.memset, QaT.ap, 0.0, writes=alltk(QaT, 8))
            fw.op(dve, V.memset, QbT.ap[96:128, :, :], 0.0, writes=alltk(QbT, 8))
            cqnT = ar.alloc("cqnTs", [3, TS], BF16)
            def chain_sqa(ti, t0, n, QaT=QaT):
                yield from even_qa(hTs.ap, hTs.t(), t0, n, QaT, t0, ropeA=(ropeA_ext.ap[:n, ti, :], ropeA_ext.t()))

            def chain_sqb(ti, t0, n, cqnT=cqnT, QbT=QbT):
                yield from even_cq(hTs.ap, hTs.t(), t0, n, cqnT, t0)
                yield from even_qb(cqnT, t0, n, QbT, ropeB=(ropeB_ext.ap[:n, ti, :], ropeB_ext.t()))
            gl_ = []
            for ti, (t0, n) in enumerate(ttiles(TS)):
                gl_ += [chain_sqb(ti, t0, n), chain_sqa(ti, t0, n)]
            run_lanes(gl_, width=4)
            cqnT.free()
            hTs.free()
            wine.free()
            wuq.free()
            ropeA_ext.free()
            ropeB_ext.free()
            winekv = ar.alloc("winekv", [8, 544], BF16)
            for kc in range(8):
                fw.dma(pool, winekv.ap[:, kc, 0:256], d_wine[kc * 128:(kc + 1) * 128, 512:768], writes=[winekv.t()])
                fw.dma(pool, winekv.ap[:, kc, 256:544], d_wine[kc * 128:(kc + 1) * 128, 1152:1440], writes=[winekv.t()])
            KaT2 = ar.alloc("KaT2s", [2, NK0], BF16)
            Va = ar.alloc("Vas", [NK0 // 128, 2, 128], BF16)
            ckvnT = ar.alloc("ckvnTs", [2, NK0], BF16)
            kpeT2 = ar.alloc("kpeT2s", [NK0], BF16)
            ones_cols(Va, NK0 // 128, 2)
            ropeA_full = ar.alloc("ropeA_full", [16, 64], F32)
            ropeB_full = ar.alloc("ropeB_full", [16, 32], F32)
            fw.dma(sp, ropeA_full.ap, d_ropeA_full, writes=[ropeA_full.t()])
            fw.dma(sp, ropeB_full.ap, d_ropeB_full, writes=[ropeB_full.t()])
            FB = 128
            for li_ in (2, 3):
                if not lanes[li_].live:
                    lane_alloc(lanes[li_], li_)
            for L in lanes[:4]:
                L.xb = ar.alloc(uniq("gfx"), [1024], F32)
                L.xn = ar.alloc(uniq("gfn"), [1024], BF16)
                L.hfT = ar.alloc(uniq("hfT"), [8, FB], BF16)

            def chain_gf(blk, KaT2=KaT2, Va=Va, ckvnT=ckvnT, kpeT2=kpeT2):
                xb, xn, hfT = S.xb, S.xn, S.hfT
                fw.dma(sp, xb.ap, d_xf[blk * FB:(blk + 1) * FB, :], writes=[xb.t()])
                yield
                fw.op(dve, V.memset, S.ssb.ap[:, 9:10], 0.0, writes=[S.ssb.t()])
                yield
                fw.op(act, A.activation, xn.ap, xb.ap, AF.Square, accum_out=S.ssb.ap[:, 9:10],
                      reads=[xb.t()], writes=[xn.t(), S.ssb.t()])
                yield
                rstd_from_ss(S.ssb.ap[:, 9:10], FB, 1, 1.0 / 1024, S.ssb.t())
                yield
                fw.op(dve, V.tensor_scalar_mul, xn.ap, xb.ap, S.ssb.ap[:, 9:10], reads=[xb.t(), S.ssb.t()], writes=[xn.t()])
                yield
                bks = []
                for g in range(2):
                    bk, bt = bank("s")
                    bkb = bk[:, :].bitcast(BF16)
                    for c4 in range(4):
                        c = g * 4 + c4
                        fw.op(pe, nc.tensor.transpose, bkb[:, c4 * 128:(c4 + 1) * 128], xn.ap[:, c * 128:(c + 1) * 128],
                              identb.ap, reads=[xn.t(), identb.t()], writes=[bt])
                        yield
                    bks.append((bkb, bt))
                for c in range(8):
                    bkb, bt = bks[c // 4]
                    src = bkb[:, (c % 4) * 128:(c % 4 + 1) * 128]
                    if c % 2 == 0:
                        fw.op(act, A.activation, hfT.ap[:, c, :], src, AF.Identity, bias=mod_sh(0, 0, c, 1),
                              scale=gm[0][0].ap[:, c, 1:2], reads=[bt, mod[0].t(0), gm[0][0].t()], writes=[hfT.t()])
                    else:
                        fw.op(dve, V.tensor_scalar, hfT.ap[:, c, :], src, gm[0][0].ap[:, c, 1:2], mod_sh(0, 0, c, 1),
                              ALU.mult, ALU.add, reads=[bt, mod[0].t(0), gm[0][0].t()], writes=[hfT.t()])
                    yield
                gt_ = blk
                yield from even_kv_side(("proj", hfT.ap, hfT.t(), 0, winekv, 0, 256), FB, KaT2, Va, ckvnT, kpeT2,
                                        blk * FB, gt_,
                                        ropeA=(ropeA_full.ap[:, gt_, :], ropeA_full.t()),
                                        ropeB=(ropeB_full.ap[:, gt_, :], ropeB_full.t()))
            run_lanes([chain_gf(blk) for blk in range(NFULL // FB)], width=4)
            for L in lanes[:4]:
                L.xb.free()
                L.xn.free()
                L.hfT.free()
            lane_free(lanes[2])
            lane_free(lanes[3])
            wukv = load_w("wukv", d_wukv, 2, 1024)
            ropeA_full.free()
            ropeB_full.free()
            winekv.free()
            def chain_ctx0(ti, KaT2=KaT2, Va=Va, ckvnT=ckvnT, kpeT2=kpeT2):
                kf = S.out_stage()
                r0 = ti * 128
                fw.dma(sp, kf.ap[:, 0:128], d_cak[r0:r0 + 128, :], writes=[kf.t()])
                fw.dma(sp, kf.ap[:, 128:256], d_cav[r0:r0 + 128, :], writes=[kf.t()])
                fw.dma(sp, kf.ap[:, 256:512], d_cbckv[r0:r0 + 128, :], writes=[kf.t()])
                fw.dma(sp, kf.ap[:, 512:544], d_cbkpe[r0:r0 + 128, :], writes=[kf.t()])
                yield from even_kv_side(("cache", kf), 128, KaT2, Va, ckvnT, kpeT2, NFULL + r0, (NFULL + r0) // 128)
            run_lanes([chain_ctx0(ti) for ti in range(NCTX // 128)], width=4)
            OT = ar.alloc("OTs", [8, TS], BF16)
            wo_box = {}

            def after_A(QaT=QaT, KaT2=KaT2, Va=Va):
                QaT.free()
                KaT2.free()
                Va.free()
                wo_box["w"] = load_w("woute", d_woute, 8, 1024)
            even_attention(TS, QaT, QbT, KaT2, Va, ckvnT, kpeT2, NK0, OT, 0, [(0, TS, 0, NK0)], act_recip=False,
                           after_A=after_A)
            wukv.free()
            for b_ in (QbT, ckvnT, kpeT2):
                b_.free()
        else:
            hTs.free()
            wine.free()
            wuq.free()

        if STAGE >= 2:
            gs = dict(T=TS, ooff=0, xT=xTs.ap, xt=xTs.t(), j=1)
            woute = wo_box["w"]
            out_proj(0, woute, OT, [gs])
            woute.free()
            OT.free()

        ffn_prefetch(0)
        hTp = ar.alloc("h2Tp", [8, TP], BF16)
        ng_ = [norm_mod(xTp.ap, xTp.t(), TP, 0, 1, 0, hTp.ap, hTp.t())]
        fg = [dict(hT=hTp.ap, ht=hTp.t(), T=TP, xT=xTp.ap, xt=xTp.t(), j=0, segs=[(0, SEQ), (SEQ, 2 * SEQ)])]
        if STAGE >= 2:
            hTs = ar.alloc("h2Ts", [8, TS], BF16)
            vmask = ar.alloc("vmask", [TS], F32)
            vmbox["v"] = vmask
            fw.dma(sp, vmask.ap, d_vmask, writes=[vmask.t()])
            ng_.append(norm_mod(xTs.ap, xTs.t(), TS, 0, 1, 1, hTs.ap, hTs.t(), mask=vmask.ap))
        run_lanes(ng_)
        if STAGE >= 2:
            fg.append(dict(hT=hTs.ap, ht=hTs.t(), T=TS, xT=xTs.ap, xt=xTs.t(), j=1, segs=[(0, TS)]))
        ffn(0, fg)
        hTp.free()
        if STAGE >= 2:
            hTs.free()

        wino = load_w("wino", d_wino, 8, 1280)
        hTp = ar.alloc("h1Tp", [8, TP], BF16)
        drain(norm_mod(xTp.ap, xTp.t(), TP, 1, 0, 0, hTp.ap, hTp.t()))
        OT = ar.alloc("OT1", [8, TP + TQ1], BF16)
        SC_C = 64 ** -0.5

        def odd_q(hT_ap, h_t, t0, n, QcT, tok_off, ropeA=None, halves=(0, 1)):
            for half in halves:
                bk, bt = proj_tm(hT_ap, h_t, t0, n, wino, half * 512, (half + 1) * 512)
                yield
                if ropeA is not None:
                    yield from rope_tm(None, bt, n, 8, 64, ropeA[0], ropeA[1], S.scr,
                                       view3=bk[:n, 0:512].rearrange("p (h d) -> p h d", h=8),
                                       dst3=S.tmbf.ap[:n, 0:512].rearrange("p (h d) -> p h d", h=8), dst_t=S.tmbf.t(),
                                       on_dve=True)
                else:
                    fw.op(act, A.copy, S.tmbf.ap[:n, 0:512], bk[:n, 0:512], reads=[bt], writes=[S.tmbf.t()])
                    yield
                yield from transpose_to(None, QcT.t(tok_off // 128), S.tmbf.ap, S.tmbf.t(), n, 4,
                                        group_fn=lambda c0, g, half=half: [
                                            (QcT.ap[0:64, half * 8 + 2 * c0:half * 8 + 2 * (c0 + g):2, tok_off:tok_off + n], 0, 64),
                                            (QcT.ap[64:128, half * 8 + 2 * c0 + 1:half * 8 + 2 * (c0 + g):2, tok_off:tok_off + n], 64, 128)])

        def odd_kv(src, n, KcT2, Vc, key_off, vtile, outs=None, ropeA=None, valid=None):
            if src[0] == "proj":
                _, hT_ap, h_t, t0 = src
                bk, bt = proj_tm(hT_ap, h_t, t0, n, wino, 1024, 1280)
                yield
                kf = S.out_stage()
                fw.op(act, A.copy, kf.ap[:n, 0:256], bk[:n, 0:256], reads=[bt], writes=[kf.t()])
                yield
                if outs is not None:
                    fw.dma(sp, o_ck[outs:outs + n, :], kf.ap[:n, 0:128], reads=[kf.t()], is_output=True)
                    yield
                    fw.dma(sp, o_cv[outs:outs + n, :], kf.ap[:n, 128:256], reads=[kf.t()], is_output=True)
                    yield
                if ropeA is not None:
                    yield from rope_tm(kf.ap[:n, 0:128], kf.t(), n, 2, 64, ropeA[0], ropeA[1], S.scr)
            else:
                _, kf = src
            fw.op(dve, V.tensor_copy, S.tmbf.ap[:n, 0:256].rearrange("p (k r d) -> p k r d", k=2, r=2),
                  kf.ap[:n, 0:128].rearrange("p (k d) -> p k d", k=2).unsqueeze(2).broadcast_to([n, 2, 2, 64]),
                  reads=[kf.t()], writes=[S.tmbf.t()])
            yield
            yield from transpose_to(None, KcT2.t(key_off // 128), S.tmbf.ap, S.tmbf.t(), n, 2,
                                    group_fn=lambda c0, g: [(KcT2.ap[:, c0:c0 + g, key_off:key_off + n], 0, 128)])
            if valid is None:
                fw.op(act, A.copy, Vc.ap[:n, vtile, :, 0:64], kf.ap[:n, 128:256].rearrange("p (k d) -> p k d", k=2),
                      reads=[kf.t()], writes=[Vc.t(vtile)])
                yield
            else:
                fw.op(dve, V.tensor_scalar_mul, Vc.ap[:n, vtile, :, 0:64], kf.ap[:n, 128:256].rearrange("p (k d) -> p k d", k=2),
                      valid, reads=[kf.t(), consts.t()], writes=[Vc.t(vtile)])
                yield
                fw.op(dve, V.tensor_scalar_mul, Vc.ap[:n, vtile, :, 64:128], Vc.ap[:n, vtile, :, 64:128], valid,
                      reads=[Vc.t(vtile), consts.t()], writes=[Vc.t(vtile)])
                yield

        QcT = ar.alloc("QcTp", [16, TP], BF16)
        fw.op(dve, V.memset, QcT.ap, 0.0, writes=alltk(QcT, 8))
        KcT2 = ar.alloc("KcT2p", [2, TP], BF16)
        Vc = ar.alloc("Vcp", [4, 2, 128], BF16)
        ones_cols(Vc, 4, 2)
        def chain_p1q(ti, t0, n, half, QcT=QcT):
            yield from odd_q(hTp.ap, hTp.t(), t0, n, QcT, t0, halves=(half,))

        def chain_p1k(ti, t0, n, KcT2=KcT2, Vc=Vc):
            yield from odd_kv(("proj", hTp.ap, hTp.t(), t0), n, KcT2, Vc, t0, ti, outs=t0)
        gl_ = []
        for ti, (t0, n) in enumerate(ttiles(TP)):
            gl_ += [chain_p1q(ti, t0, n, 0), chain_p1q(ti, t0, n, 1), chain_p1k(ti, t0, n)]
        run_lanes(gl_, width=4)
        for h in range(16):
            kvh = h // 8
            pb = (h % 2) * 64
            for s in range(2):
                q0 = s * SEQ
                attend(SEQ,
                       lambda ki, kvh=kvh, pb=pb, h=h, q0=q0: (
                           [(KcT2.ap[:, kvh, q0 + ki * 128:q0 + ki * 128 + 128],
                             QcT.ap[:, h, q0:q0 + SEQ])], [KcT2.t((q0 + ki * 128) // 128)]),
                       [128, 128],
                       lambda ki, kvh=kvh, s=s: (Vc.ap[:, s * 2 + ki, kvh, :], [Vc.t(s * 2 + ki)]),
                       SC_C, OT.ap[pb:pb + 64, h // 2, q0:q0 + SEQ], OT.t(h // 2), alltk(QcT, 8), sink_col=h)
        flush_attends()
        for b_ in (QcT, KcT2, Vc, hTp):
            b_.free()
        wouto = load_w("wouto", d_wouto, 8, 1024)

        if STAGE >= 2:
            hTs = ar.alloc("h1Ts", [8, TS], BF16)
            drain(norm_mod(xTs.ap, xTs.t(), TS, 1, 0, 1, hTs.ap, hTs.t()))
            ropeA_q1 = ar.alloc("ropeA_q1", [5, 64], F32)
            ropeA_k1 = ar.alloc("ropeA_k1", [NK1T, 64], F32)
            fw.dma(sp, ropeA_q1.ap, d_ropeA_q1, writes=[ropeA_q1.t()])
            fw.dma(sp, ropeA_k1.ap, d_ropeA_k1, writes=[ropeA_k1.t()])
            band = ar.alloc("band", [NK1T, TQ1], BF16)
            fw.dma(pool, band.ap, d_band, writes=[band.t()])
            NKC = NK1T * K1T
            KcT2 = ar.alloc("KcT2s", [2, NKC + NCTX], BF16)
            Vc = ar.alloc("Vcs", [NK1T + 4, 2, 128], BF16)
            ones_cols(Vc, NK1T + 4, 2)
            QcT = ar.alloc("QcTs", [16, TQ1], BF16)
            fw.op(dve, V.memset, QcT.ap, 0.0, writes=alltk(QcT, 8))
            def chain_k1(jt, KcT2=KcT2, Vc=Vc):
                t0 = K1OFF + jt * K1T
                yield from odd_kv(("proj", hTs.ap, hTs.t(), t0), K1T, KcT2, Vc, jt * K1T, jt,
                                  ropeA=(ropeA_k1.ap[:K1T, jt, :], ropeA_k1.t()), valid=cst("validk1", jt, jt + 1)[:K1T, :])

            def chain_c1(ti, KcT2=KcT2, Vc=Vc):
                kf = S.out_stage()
                r0 = ti * 128
                fw.dma(sp, kf.ap[:, 0:128], d_cck[r0:r0 + 128, :], writes=[kf.t()])
                fw.dma(sp, kf.ap[:, 128:256], d_ccv[r0:r0 + 128, :], writes=[kf.t()])
                yield from odd_kv(("cache", kf), 128, KcT2, Vc, NKC + r0, NK1T + ti)

            def chain_q1(ti, q0, n, half, QcT=QcT):
                yield from odd_q(hTs.ap, hTs.t(), Q1OFF + q0, n, QcT, q0, ropeA=(ropeA_q1.ap[:n, ti, :], ropeA_q1.t()),
                                 halves=(half,))
            run_lanes(width=4, gens=[chain_k1(jt) for jt in range(NK1T)] + [chain_c1(ti) for ti in range(4)] +
                      [chain_q1(ti, q0, n, hf) for ti, (q0, n) in enumerate(ttiles(TQ1)) for hf in (0, 1)])
            hTs.free()
            halves = [((0, 257), [0, 1, 2, 3, 4]), ((257, 514), [2, 3, 4, 5, 6])]
            for h in range(16):
                kvh = h // 8
                pb = (h % 2) * 64
                for (qa, qb), jts in halves:
                    kl = [("b", jt) for jt in jts] + [("c", ci) for ci in range(4)]

                    def s_pairs(ki, kl=kl, kvh=kvh, pb=pb, h=h, qa=qa, qb=qb):
                        kind, idx = kl[ki]
                        if kind == "b":
                            kap = KcT2.ap[:, kvh, idx * K1T:(idx + 1) * K1T]
                            ktk = KcT2.t((idx * K1T) // 128)
                        else:
                            kap = KcT2.ap[:, kvh, NKC + idx * 128:NKC + (idx + 1) * 128]
                            ktk = KcT2.t((NKC + idx * 128) // 128)
                        return [(kap, QcT.ap[:, h, qa:qb])], [ktk]

                    def v_f(ki, kl=kl, kvh=kvh):
                        kind, idx = kl[ki]
                        if kind == "b":
                            return Vc.ap[:K1T, idx, kvh, :], [Vc.t(idx)]
                        return Vc.ap[:, NK1T + idx, kvh, :], [Vc.t(NK1T + idx)]

                    def m_f(ki, kl=kl, qa=qa, qb=qb):
                        kind, idx = kl[ki]
                        if kind == "b":
                            return band.ap[:K1T, idx, qa:qb], band.t()
                        return None

                    attend(qb - qa, s_pairs, [K1T if k[0] == "b" else 128 for k in kl], v_f, SC_C,
                           OT.ap[pb:pb + 64, h // 2, TP + qa:TP + qb], OT.t(h // 2), alltk(QcT, 8), mask_fn=m_f, sink_col=h)
            flush_attends()
            for b_ in (QcT, KcT2, Vc, band, ropeA_q1, ropeA_k1):
                b_.free()
        wino.free()

        xs1 = xTs.ap[:, :, Q1OFF:Q1OFF + TQ1]
        gp = dict(T=TP, ooff=0, xT=xTp.ap, xt=xTp.t(), j=0)
        gs = dict(T=TQ1, ooff=TP, xT=xs1, xt=xTs.t(), j=1)
        out_proj(1, wouto, OT, [gp] + ([gs] if STAGE >= 2 else []))
        wouto.free()
        OT.free()

        ffn_prefetch(1)
        hTp = ar.alloc("h2Tp1", [8, TP], BF16)
        ng_ = [norm_mod(xTp.ap, xTp.t(), TP, 1, 1, 0, hTp.ap, hTp.t())]
        fg = [dict(hT=hTp.ap, ht=hTp.t(), T=TP, xT=xTp.ap, xt=xTp.t(), j=0, segs=[(0, SEQ), (SEQ, 2 * SEQ)])]
        if STAGE >= 2:
            hTs = ar.alloc("h2Ts1", [8, TQ1], BF16)
            ng_.append(norm_mod(xs1, xTs.t(), TQ1, 1, 1, 1, hTs.ap, hTs.t(), mask=vmask.ap[:, Q1OFF:Q1OFF + TQ1]))
        run_lanes(ng_)
        if STAGE >= 2:
            fg.append(dict(hT=hTs.ap, ht=hTs.t(), T=TQ1, xT=xs1, xt=xTs.t(), j=1, segs=[(0, TQ1)]))
        ffn(1, fg)
        hTp.free()
        if STAGE >= 2:
            hTs.free()

        def final_out(xsrc, xt, T, d_out):
            yT = ar.alloc(uniq("yT"), [8, T], F32)
            yield from norm_mod(xsrc, xt, T, 0, 0, 0, yT.ap, yT.t(), plain_gain=cst("gfin"))
            ring = [ar.alloc(uniq("yout"), [1024], F32) for i in range(2)]
            for it, (t0, n) in enumerate(ttiles(T)):
                yb = ring[it % 2]
                for g in range(2):
                    bk, bt = bank("s")
                    for c4 in range(4):
                        c = g * 4 + c4
                        fw.op(pe, nc.tensor.transpose, bk[:n, c4 * 128:(c4 + 1) * 128], yT.ap[:, c, t0:t0 + n],
                              ident.ap, reads=[yT.t(), ident.t()], writes=[bt])
                    evac_copy(yb.ap[:n, g * 512:(g + 1) * 512], bk[:n, :], [bt], [yb.t()])
                    yield
                fw.dma(sp, d_out[t0:t0 + n, :], yb.ap[:n, :], reads=[yb.t()], is_output=True)
                yield
            for b_ in ring:
                b_.free()
            yT.free()

        fo_ = [final_out(xTp.ap, xTp.t(), TP, o_yp)]
        if STAGE >= 2:
            fo_.append(final_out(xTs.ap[:, :, OWNOFF:OWNOFF + 512], xTs.t(), 512, o_ys))
        run_lanes(fo_)
        fw.finish()
        build_program.stats = dict(peak=ar.peak, nsem=fw.nsem,
                                   ninst={e.name: e.ninst for e in (pe, act, dve, pool, sp)})
    return nc


def _prep_shared(inp):
    f = lambda k: np.asarray(inp[k], np.float32)
    sh = {}
    wm = f("w_mod").reshape(2, 8, 128, 12, 512)
    sh["w_mod_r"] = np.ascontiguousarray(wm.transpose(0, 3, 2, 1, 4))
    sh["w_in_e"] = np.ascontiguousarray(f("w_in_e")[0])
    sh["w_uq"] = np.ascontiguousarray(f("w_uq_b")[0])
    wukv = f("w_ukv_b")[0].reshape(256, 8, 128)
    sh["w_ukv"] = np.ascontiguousarray(np.concatenate([wukv[:, :, :64].reshape(256, 512), wukv[:, :, 64:].reshape(256, 512)], axis=1))
    sh["w_out_e"] = np.ascontiguousarray(f("w_out_e")[0])
    sh["w_in_o"] = np.ascontiguousarray(f("w_in_o")[0])
    sh["w_out_o"] = np.ascontiguousarray(f("w_out_o")[0])
    wup = f("w_up")
    g = wup[:, :, :DFF].reshape(2, 8, 128, NJ, 128)
    v = wup[:, :, DFF:].reshape(2, 8, 128, NJ, 128)
    gv = np.concatenate([g, v], axis=-1)
    sh["w_up_r"] = np.ascontiguousarray(gv.transpose(0, 3, 2, 1, 4))
    wdn = f("w_down").reshape(2, NJ, 128, 8, 128)
    sh["w_down_r"] = np.ascontiguousarray(wdn.transpose(0, 3, 2, 1, 4))
    p = np.arange(128)[:, None, None]
    jt = np.arange(NK1T)[None, :, None]
    qi = np.arange(TQ1)[None, None, :]
    d = K1T * jt + p - 128 - qi
    band = ((np.abs(d) <= 128) & (p < K1T)).astype(np.float32)
    sh["band"] = np.ascontiguousarray(band)
    full_pos = np.arange(NFULL)
    sh["ropeA_full"] = _tile_rows(_rope_tables(full_pos, 64), 128, 16)
    sh["ropeB_full"] = _tile_rows(_rope_tables(full_pos, 32), 128, 16)
    return sh


def _prep_core(inp, i):
    f = lambda k: np.asarray(inp[k], np.float32)
    sb, r = i // 4, i % 4
    s = 512 * r
    m = {}
    m["xp"] = np.ascontiguousarray(f("x_prompt")[2 * i:2 * i + 2].reshape(TP, 1024))
    pos = np.arange(s - HALO, s - HALO + TS)
    ok = (pos >= 0) & (pos < NFULL)
    xs = np.zeros((TS, 1024), np.float32)
    xs[ok] = f("x_sample")[sb][pos[ok]]
    m["xs"] = xs
    m["xf"] = np.ascontiguousarray(f("x_sample")[sb])
    m["ca_k"] = np.ascontiguousarray(f("cache_a_k")[sb, 0].reshape(NCTX, 128))
    m["ca_v"] = np.ascontiguousarray(f("cache_a_v")[sb, 0].reshape(NCTX, 128))
    m["cb_ckv"] = np.ascontiguousarray(f("cache_b_ckv")[sb, 0])
    m["cb_kpe"] = np.ascontiguousarray(f("cache_b_kpe")[sb, 0])
    m["cc_k"] = np.ascontiguousarray(f("cache_c_k")[sb, 0].reshape(NCTX, 128))
    m["cc_v"] = np.ascontiguousarray(f("cache_c_v")[sb, 0].reshape(NCTX, 128))
    cs = np.zeros((128, NCONST), np.float32)

    def put(name, arr):
        lo, hi = CL[name]
        cs[:, lo:hi] = np.asarray(arr, np.float32).reshape(128, hi - lo)
    for l in range(2):
        put("gmix%d" % l, _fm(f("g_mix_norm")[l], 8))
        put("gffn%d" % l, _fm(f("g_ffn_norm")[l], 8))
        put("bmod%d" % l, _fm(f("b_mod")[l], 48))
        cw = f("conv_w")[l].reshape(3, NJ, 128).transpose(2, 1, 0)
        put("convw%d" % l, np.ascontiguousarray(cw))
        put("convb%d" % l, _fm(f("conv_b")[l], NJ))
    put("gfin", _fm(f("g_final"), 8))
    cond = np.stack([_fm(f("c_ctx"), 8), _fm(f("c")[sb], 8)], axis=-1)
    put("condT", cond)
    put("sink", _bc(f("sink_c")[0]))
    put("gq", _bc(f("g_qnorm_a")[0]))
    put("gk", _bc(f("g_knorm_a")[0]))
    put("gcq", _bc(f("g_cq_b")[0]))
    put("gckv", _bc(f("g_ckv_b")[0]))
    vk = np.zeros((128, NK1T), np.float32)
    for jt in range(NK1T):
        kp = s - HALO + K1OFF + jt * K1T + np.arange(K1T)
        vk[:K1T, jt] = ((kp >= 0) & (kp < NFULL)).astype(np.float32)
    put("validk1", vk)
    m["consts"] = cs
    m["vmask"] = np.ascontiguousarray(np.broadcast_to(ok.astype(np.float32)[None, :], (128, TS)))
    m["ropeA_ext"] = _tile_rows(_rope_tables(pos, 64), 128, 7)
    m["ropeB_ext"] = _tile_rows(_rope_tables(pos, 32), 128, 7)
    m["ropeA_q1"] = _tile_rows(_rope_tables(pos[Q1OFF:Q1OFF + TQ1], 64), 128, 5)
    m["ropeA_k1"] = _tile_rows(_rope_tables(pos[K1OFF:K1OFF + NK1T * K1T], 64), K1T, NK1T)
    return m


_NC_CACHE = {}


def kernel(**inputs):
    if "nc" not in _NC_CACHE:
        _NC_CACHE["nc"] = build_program()
    nc = _NC_CACHE["nc"]
    shared = _prep_shared(inputs)
    in_maps = []
    for i in range(NCORES):
        m = _prep_core(inputs, i)
        m.update(shared)
        in_maps.append(m)
    res = run_bass_kernel_spmd(nc, in_maps, core_ids=list(range(NCORES)))
    R = res.results
    y_p = np.stack([R[i]["y_p"].reshape(2, SEQ, 1024) for i in range(NCORES)]).reshape(16, SEQ, 1024)
    y_s = np.stack([R[i]["y_s"] for i in range(NCORES)]).reshape(2, NFULL, 1024)

    def cat(name, shp):
        return np.stack([R[i][name].reshape((2, SEQ) + shp) for i in range(NCORES)]).reshape((16, 1, SEQ) + shp)
    outs = (y_p, y_s, cat("n_ak", (2, 64)), cat("n_av", (2, 64)), cat("n_ckv", (256,)), cat("n_kpe", (32,)),
            cat("n_ck", (2, 64)), cat("n_cv", (2, 64)))
    return tuple(np.ascontiguousarray(o.astype(np.float32)) for o in outs)
```

```python
import os
from contextlib import ExitStack

import numpy as np
import concourse.bass as bass
import concourse.mybir as mybir
from concourse.bass_utils import run_bass_kernel_spmd

F32 = mybir.dt.float32
BF16 = mybir.dt.bfloat16
U8 = mybir.dt.uint8
AF = mybir.ActivationFunctionType
ALU = mybir.AluOpType
AX = mybir.AxisListType

EPS = 1e-6
NCORES = 8
TP = 512
SEQ = 256
HALO = 130
TS = 512 + 2 * HALO
Q1OFF = 129
TQ1 = 514
K1OFF = 1
K1T = 110
NK1T = 7
OWNOFF = 130
NFULL = 2048
NCTX = 512
NK0 = NFULL + NCTX
DFF = 2816
NJ = 22
SEM_LIMIT = 20000

STAGE = int(os.environ.get("KSTAGE", "99"))


def cdiv(a, b):
    return (a + b - 1) // b


def nchunks(T, maxn=512):
    n = cdiv(T, maxn)
    sz = cdiv(T, n)
    return [(i * sz, min(T, (i + 1) * sz)) for i in range(n)]


def ttiles(T, tsz=128):
    return [(i, min(tsz, T - i)) for i in range(0, T, tsz)]


class Trk:
    __slots__ = ("w", "r")

    def __init__(self, inherit=None):
        self.w = None
        self.r = list(inherit) if inherit else []


def _dedupe(evs):
    d = {}
    for e in evs:
        k = id(e[0])
        if k not in d or d[k][1] < e[1]:
            d[k] = e
    return list(d.values())


class Eng:
    def __init__(self, fw, name, h):
        self.fw = fw
        self.name = name
        self.h = h
        self.sems = []
        self.cnt = 0
        self.seen = {}
        self.ninst = 0

    def cur_sem(self):
        if not self.sems or self.cnt >= SEM_LIMIT:
            self.sems.append(self.fw.new_sem("e_%s_%d" % (self.name, len(self.sems))))
            self.cnt = 0
        return self.sems[-1]


class FW:
    def __init__(self, nc, stack):
        self.nc = nc
        self.stack = stack
        self.nsem = 0
        self.pe = Eng(self, "pe", nc.tensor)
        self.act = Eng(self, "act", nc.scalar)
        self.dve = Eng(self, "dve", nc.vector)
        self.pool = Eng(self, "pool", nc.gpsimd)
        self.sp = Eng(self, "sp", nc.sync)
        self.dma_pools = {}
        self.out_events = []
        self.flip = 0

    def new_sem(self, name):
        self.nsem += 1
        return self.stack.enter_context(self.nc.semaphore(name))

    def _wait(self, eng, ev):
        sem, val = ev
        k = id(sem)
        if eng.seen.get(k, 0) >= val:
            return
        eng.seen[k] = val
        eng.h.wait_ge(sem, val)
        eng.ninst += 1

    def _skip(self, eng, ev):
        return eng is self.pe and any(ev[0] is s for s in eng.sems)

    def _deps(self, eng, reads, writes):
        for t in reads:
            if t.w is not None and not self._skip(eng, t.w):
                self._wait(eng, t.w)
        for t in writes:
            if t.w is not None and not self._skip(eng, t.w):
                self._wait(eng, t.w)
            for ev in t.r:
                if not self._skip(eng, ev):
                    self._wait(eng, ev)

    def _needs(self, eng, reads, writes, acc):
        def add(ev):
            if self._skip(eng, ev):
                return
            k = id(ev[0])
            if k not in acc or acc[k][1] < ev[1]:
                acc[k] = ev
        for t in reads:
            if t.w is not None:
                add(t.w)
        for t in writes:
            if t.w is not None:
                add(t.w)
            for ev in t.r:
                add(ev)

    def mm_burst(self, groups):
        eng = self.pe
        acc = {}
        for (out, mms, reads, writes) in groups:
            self._needs(eng, reads, writes, acc)
        for ev in acc.values():
            self._wait(eng, ev)
        evs = []
        for (out, mms, reads, writes) in groups:
            ins = None
            for (l, r, st_, sp_) in mms:
                ins = self.nc.tensor.matmul(out, l, r, start=st_, stop=sp_)
                eng.ninst += 1
            sem = eng.cur_sem()
            eng.cnt += 1
            ins.then_inc(sem, 1)
            ev = (sem, eng.cnt)
            self._mark(ev, reads, writes)
            evs.append(ev)
        return evs

    def _mark(self, ev, reads, writes):
        for t in reads:
            t.r.append(ev)
            if len(t.r) > 12:
                t.r = _dedupe(t.r)
        for t in writes:
            t.w = ev
            t.r = []

    def op(self, eng, fn, *args, reads=(), writes=(), **kw):
        self._deps(eng, reads, writes)
        ins = fn(*args, **kw)
        sem = eng.cur_sem()
        eng.cnt += 1
        ins.then_inc(sem, 1)
        eng.ninst += 1
        ev = (sem, eng.cnt)
        self._mark(ev, reads, writes)
        return ev

    def mm_group(self, out, pairs, reads, writes, tr=False):
        eng = self.pe
        self._deps(eng, reads, writes)
        n = len(pairs)
        ins = None
        for i, (l, r) in enumerate(pairs):
            ins = self.nc.tensor.matmul(out, l, r, start=(i == 0), stop=(i == n - 1))
            eng.ninst += 1
        sem = eng.cur_sem()
        eng.cnt += 1
        ins.then_inc(sem, 1)
        ev = (sem, eng.cnt)
        self._mark(ev, reads, writes)
        return ev

    def dma(self, q, out, in_, reads=(), writes=(), is_output=False):
        pool = self.dma_pools.setdefault(q.name, {"sems": [], "vals": [], "i": 0})
        NP = 12
        if len(pool["sems"]) < NP:
            pool["sems"].append(self.new_sem("d_%s_%d" % (q.name, len(pool["sems"]))))
            pool["vals"].append(0)
            i = len(pool["sems"]) - 1
        else:
            i = pool["i"] % NP
            self._wait(q, (pool["sems"][i], pool["vals"][i]))
        pool["i"] += 1
        self._deps(q, reads, writes)
        sem = pool["sems"][i]
        pool["vals"][i] += 16
        ins = q.h.dma_start(out=out, in_=in_)
        ins.then_inc(sem, 16)
        q.ninst += 1
        ev = (sem, pool["vals"][i])
        self._mark(ev, reads, writes)
        if is_output:
            self.out_events.append(ev)
        return ev

    def finish(self):
        for ev in _dedupe(self.out_events):
            self._wait(self.sp, ev)
        for e in (self.pe, self.act, self.dve, self.pool):
            if e.sems:
                self._wait(self.sp, (e.sems[-1], e.cnt))


class Buf:
    def __init__(self, arena, name, off, nbytes, ap, inherit):
        self.arena = arena
        self.name = name
        self.off = off
        self.nbytes = nbytes
        self.ap = ap
        self.inherit = inherit
        self.trks = {}

    def t(self, key=0):
        if key not in self.trks:
            self.trks[key] = Trk(self.inherit)
        return self.trks[key]

    def free(self):
        self.arena.free(self)


class Arena:
    def __init__(self, tensor, size):
        self.tensor = tensor
        self.size = size
        self.freelist = [(0, size)]
        self.dead = []
        self.peak = 0
        self.live = {}

    def alloc(self, name, free_shape, dtype):
        esz = {F32: 4, BF16: 2, U8: 1}[dtype]
        n = 1
        for s in free_shape:
            n *= s
        nbytes = cdiv(n * esz, 64) * 64
        for i, (a, b) in enumerate(self.freelist):
            if b - a >= nbytes:
                off = a
                if b - a == nbytes:
                    self.freelist.pop(i)
                else:
                    self.freelist[i] = (a + nbytes, b)
                break
        else:
            raise RuntimeError("SBUF arena OOM allocating %s (%d B); live=%s" % (
                name, nbytes, {k: v for k, v in self.live.items()}))
        inherit = []
        nd = []
        for (a, b, evs) in self.dead:
            if a < off + nbytes and off < b:
                inherit.extend(evs)
                if a < off:
                    nd.append((a, off, evs))
                if b > off + nbytes:
                    nd.append((off + nbytes, b, evs))
            else:
                nd.append((a, b, evs))
        self.dead = nd
        inherit = _dedupe(inherit)
        ap = self.tensor[:, off:off + n * esz]
        if dtype != U8:
            ap = ap.bitcast(dtype)
        if len(free_shape) == 2:
            ap = ap.rearrange("p (a b) -> p a b", a=free_shape[0])
        elif len(free_shape) == 3:
            ap = ap.rearrange("p (a b c) -> p a b c", a=free_shape[0], b=free_shape[1])
        elif len(free_shape) == 4:
            ap = ap.rearrange("p (a b c d) -> p a b c d", a=free_shape[0], b=free_shape[1], c=free_shape[2])
        self.live[name] = nbytes
        self.peak = max(self.peak, sum(self.live.values()))
        return Buf(self, name, off, nbytes, ap, inherit)

    def free(self, buf):
        evs = list(buf.inherit)
        for t in buf.trks.values():
            if t.w is not None:
                evs.append(t.w)
            evs.extend(t.r)
        self.dead.append((buf.off, buf.off + buf.nbytes, _dedupe(evs)))
        del self.live[buf.name]
        fl = self.freelist + [(buf.off, buf.off + buf.nbytes)]
        fl.sort()
        merged = []
        for a, b in fl:
            if merged and merged[-1][1] == a:
                merged[-1] = (merged[-1][0], b)
            else:
                merged.append((a, b))
        self.freelist = merged


def _rope_tables(positions, rot_dim):
    pos = np.asarray(positions, dtype=np.int64)
    ok = (pos >= 0) & (pos < NFULL)
    pc = np.where(ok, pos, 0)
    row = (pc // 64).astype(np.float64)
    col = (pc % 64).astype(np.float64)
    d_axis = rot_dim // 2
    freqs = 10000.0 ** (-np.arange(0, d_axis, 2, dtype=np.float64) / d_axis)
    ang = np.concatenate([row[:, None] * freqs, col[:, None] * freqs], axis=-1)
    out = np.concatenate([np.cos(ang), np.sin(ang)], axis=-1).astype(np.float32)
    out[~ok] = 0.0
    return out


def _tile_rows(a, tsz, ntile):
    n, f = a.shape
    out = np.zeros((128, ntile, f), np.float32)
    for t in range(ntile):
        lo = t * tsz
        hi = min(n, lo + tsz)
        if hi > lo:
            out[:hi - lo, t, :] = a[lo:hi]
    return out


def _fm(vec, nch):
    return np.ascontiguousarray(np.asarray(vec, np.float32).reshape(nch, 128).T)


def _bc(vec):
    return np.ascontiguousarray(np.broadcast_to(np.asarray(vec, np.float32)[None, :], (128, len(vec))))


def _const_layout():
    lay = {}
    off = 0

    def add(name, n):
        nonlocal off
        lay[name] = (off, off + n)
        off += n
    for l in range(2):
        add("gmix%d" % l, 8)
        add("gffn%d" % l, 8)
        add("bmod%d" % l, 48)
        add("convw%d" % l, NJ * 3)
        add("convb%d" % l, NJ)
    add("gfin", 8)
    add("condT", 16)
    add("sink", 16)
    add("gq", 64)
    add("gk", 64)
    add("gcq", 384)
    add("gckv", 256)
    add("validk1", NK1T)
    return lay, off


CL, NCONST = _const_layout()


def build_program():
    nc = bass.Bass("TRN2", target_bir_lowering=False)

    def din(name, shape):
        return nc.dram_tensor(name, list(shape), F32, kind="ExternalInput").ap()

    def dout(name, shape):
        return nc.dram_tensor(name, list(shape), F32, kind="ExternalOutput").ap()

    d_xp = din("xp", [TP, 1024])
    d_xs = din("xs", [TS, 1024])
    d_xf = din("xf", [NFULL, 1024])
    d_cak = din("ca_k", [NCTX, 128])
    d_cav = din("ca_v", [NCTX, 128])
    d_cbckv = din("cb_ckv", [NCTX, 256])
    d_cbkpe = din("cb_kpe", [NCTX, 32])
    d_cck = din("cc_k", [NCTX, 128])
    d_ccv = din("cc_v", [NCTX, 128])
    d_consts = din("consts", [128, NCONST])
    d_vmask = din("vmask", [128, TS])
    d_band = din("band", [128, NK1T, TQ1])
    d_ropeA_ext = din("ropeA_ext", [128, 7, 64])
    d_ropeB_ext = din("ropeB_ext", [128, 7, 32])
    d_ropeA_full = din("ropeA_full", [128, 16, 64])
    d_ropeB_full = din("ropeB_full", [128, 16, 32])
    d_ropeA_q1 = din("ropeA_q1", [128, 5, 64])
    d_ropeA_k1 = din("ropeA_k1", [128, NK1T, 64])
    d_wmod = din("w_mod_r", [2, 12, 128, 8, 512])
    d_wine = din("w_in_e", [1024, 1440])
    d_wuq = din("w_uq", [384, 768])
    d_wukv = din("w_ukv", [256, 1024])
    d_woute = din("w_out_e", [1024, 1024])
    d_wino = din("w_in_o", [1024, 1280])
    d_wouto = din("w_out_o", [1024, 1024])
    d_wup = din("w_up_r", [2, NJ, 128, 8, 256])
    d_wdn = din("w_down_r", [2, 8, 128, NJ, 128])

    o_yp = dout("y_p", [TP, 1024])
    o_ys = dout("y_s", [512, 1024])
    o_ak = dout("n_ak", [TP, 128])
    o_av = dout("n_av", [TP, 128])
    o_ckv = dout("n_ckv", [TP, 256])
    o_kpe = dout("n_kpe", [TP, 32])
    o_ck = dout("n_ck", [TP, 128])
    o_cv = dout("n_cv", [TP, 128])

    st = ExitStack()
    with st:
        fw = FW(nc, st)
        pe, act, dve, pool, sp = fw.pe, fw.act, fw.dve, fw.pool, fw.sp
        ARENA_BYTES = 207 * 1024
        arena_t = st.enter_context(nc.sbuf_tensor("arena", [128, ARENA_BYTES], U8))
        ar = Arena(arena_t, ARENA_BYTES)
        banks = []
        for i in range(8):
            banks.append((st.enter_context(nc.psum_tensor("bank%d" % i, [128, 512], F32)), Trk()))
        bstate = {"s": 0, "a": 0, "l0": 0, "l1": 0, "l2": 0, "l3": 0, "lanes": False, "li": 0}
        bgstate = {"gen": True, "done": 0}

        def bank(kind="s"):
            if kind == "bg":
                return banks[7]
            if kind == "a":
                i = 5 + bstate["a"] % 2
                bstate["a"] += 1
                return banks[i]
            if bstate["lanes"]:
                key = "l%d" % bstate["li"]
                i = 2 * bstate["li"] + bstate[key] % 2
                bstate[key] += 1
                return banks[i]
            spool = [0, 1, 2, 3, 4] + ([7] if bgstate["gen"] is None else [])
            i = spool[bstate["s"] % len(spool)]
            bstate["s"] += 1
            return banks[i]

        MOD_SPREAD = 3
        V = nc.vector
        A = nc.scalar
        G = nc.gpsimd

        consts = ar.alloc("consts", [NCONST], F32)
        fw.dma(sp, consts.ap, d_consts, writes=[consts.t()])

        def cst(name, a=None, b=None):
            lo, hi = CL[name]
            if a is None:
                return consts.ap[:, lo:hi]
            return consts.ap[:, lo + a:lo + b]

        ident = ar.alloc("ident", [128], F32)
        identb = ar.alloc("identb", [128], BF16)
        onesb = ar.alloc("onesb", [128], BF16)
        fw.op(pool, G.memset, ident.ap, 0.0, writes=[ident.t()])
        fw.op(pool, G.affine_select, ident.ap, ident.ap, [[-1, 128]], ALU.not_equal, 1.0, base=0,
              channel_multiplier=1, reads=[ident.t()], writes=[ident.t()])
        fw.op(pool, G.tensor_copy, identb.ap, ident.ap, reads=[ident.t()], writes=[identb.t()])
        fw.op(pool, G.memset, onesb.ap, 1.0, writes=[onesb.t()])
        epsb = ar.alloc("epsb", [16], F32)
        fw.op(pool, G.memset, epsb.ap, EPS, writes=[epsb.t()])

        class _Lane:
            pass
        NLANE = 4
        lanes = []

        def lane_alloc(L, li):
            L.tmA = ar.alloc("tmA%d" % li, [512], F32)
            L.tmB = ar.alloc("tmB%d" % li, [768], F32)
            L.tmbf = ar.alloc("tmbf%d" % li, [768], BF16)
            L.ssb = ar.alloc("ssb%d" % li, [16], F32)
            L.scr = [ar.alloc("scr%d_%d" % (li, i), [256], F32) for i in range(4)]
            L.oring = [ar.alloc("oring%d_%d" % (li, i), [576], F32) for i in range(1)]
            L.oi = 0
            L.live = True

        def lane_free(L):
            for b_ in [L.tmA, L.tmB, L.tmbf, L.ssb] + L.scr + L.oring:
                b_.free()
            L.live = False
        for li in range(NLANE):
            L = _Lane()
            L.live = False
            if li < 2:
                lane_alloc(L, li)
            lanes.append(L)

        class _Cur:
            def __getattr__(self, k):
                return getattr(lanes[self.__dict__["li"]], k)

            def out_stage(self):
                L = lanes[self.__dict__["li"]]
                b = L.oring[0]
                L.oi += 1
                return b
        S = _Cur()
        S.__dict__["li"] = 0
        ucnt = {"i": 0}

        def alltk(buf, n=24):
            return [buf.t(i) for i in range(n)]

        def uniq(name):
            ucnt["i"] += 1
            return "%s_%d" % (name, ucnt["i"])

        def run_lanes(gens, width=3, background=None):
            gens = list(gens)
            active = {}
            nxt = 0
            dyn = []
            for li in range(2, width):
                if len(gens) > li and not lanes[li].live:
                    lane_alloc(lanes[li], li)
                    dyn.append(lanes[li])
            width = sum(1 for li in range(width) if lanes[li].live)
            while nxt < len(gens) or active:
                if background:
                    pump()
                for li in range(width):
                    if li not in active and nxt < len(gens):
                        active[li] = gens[nxt]
                        nxt += 1
                for li in list(active.keys()):
                    S.__dict__["li"] = li
                    bstate["lanes"] = True
                    bstate["li"] = li
                    try:
                        next(active[li])
                    except StopIteration:
                        del active[li]
                    bstate["lanes"] = False
            for L in dyn:
                lane_free(L)
            S.__dict__["li"] = 0

        def drain(gen):
            S.__dict__["li"] = 0
            for _ in gen:
                pass

        xTp = ar.alloc("xTp", [8, TP], F32)
        xTs = ar.alloc("xTs", [8, TS], F32)

        def evac_copy(out, in_, reads, writes, force_dve=False):
            fw.flip ^= 1
            if fw.flip and not force_dve:
                return fw.op(act, A.copy, out, in_, reads=reads, writes=writes)
            return fw.op(dve, V.tensor_copy, out, in_, reads=reads, writes=writes)

        def load_xT(d_x, T, dst, dst_off=0, q=sp):
            ring = [ar.alloc(uniq("xld"), [1024], F32) for i in range(2)]
            for it, (t0, n) in enumerate(ttiles(T)):
                xb = ring[it % 2]
                fw.dma(q, xb.ap[:n, :], d_x[t0:t0 + n, :], writes=[xb.t()])
                yield
                for g in range(2):
                    bk, bt = bank("s")
                    for c4 in range(4):
                        c = g * 4 + c4
                        fw.op(pe, nc.tensor.transpose, bk[:, c4 * 128:c4 * 128 + n], xb.ap[:n, c * 128:(c + 1) * 128],
                              ident.ap[:n, :n], reads=[xb.t(), ident.t()], writes=[bt])
                        yield
                    src = bk[:, :].rearrange("p (a b) -> p a b", a=4)[:, :, :n]
                    evac_copy(dst.ap[:, g * 4:g * 4 + 4, dst_off + t0:dst_off + t0 + n], src, [bt], [dst.t()])
                    yield
            for b in ring:
                b.free()

        run_lanes([load_xT(d_xp, TP, xTp), load_xT(d_xs, TS, xTs)])

        siluc = ar.alloc("siluc", [8, 2], BF16)
        fw.op(act, A.activation, siluc.ap, cst("condT").rearrange("p (c j) -> p c j", j=2), AF.Silu,
              reads=[consts.t()], writes=[siluc.t()])
        mod = [ar.alloc("mod%d" % l, [48, 2], F32) for l in range(2)]
        gm = [[ar.alloc("gm%d_%d" % (l, k), [8, 2], F32) for k in range(2)] for l in range(2)]

        wmring = [ar.alloc("wmod%d" % i, [8, 512], BF16) for i in range(2)]
        wmstate = {"i": 0}

        def mod_gen(l, blks, spread=0):
            for blk in blks:
                wb = wmring[wmstate["i"] % 2]
                wmstate["i"] += 1
                fw.dma(pool, wb.ap, d_wmod[l, blk], writes=[wb.t()])
                yield
                bk, bt = bank("bg")
                for m in range(4):
                    fw.mm_group(bk[:, m * 2:m * 2 + 2],
                                [(wb.ap[:, kc, m * 128:(m + 1) * 128], siluc.ap[:, kc, :]) for kc in range(8)],
                                reads=[wb.t(), siluc.t()], writes=[bt])
                    yield
                v = blk // 2
                fw.op(dve, V.tensor_tensor, mod[l].ap[:, blk * 4:blk * 4 + 4, :], bk[:, 0:8].rearrange("p (c j) -> p c j", j=2),
                      cst("bmod%d" % l, blk * 4, blk * 4 + 4).unsqueeze(2).broadcast_to([128, 4, 2]), ALU.add,
                      reads=[bt, consts.t()], writes=[mod[l].t(v)])
                yield
                if blk in (3, 9):
                    k = 0 if blk == 3 else 1
                    gname, scoff = (("gmix%d" % l, 8), ("gffn%d" % l, 32))[k]
                    fw.op(dve, V.tensor_scalar_add, gm[l][k].ap, mod[l].ap[:, scoff:scoff + 8, :], 1.0,
                          reads=[mod[l].t(v)], writes=[gm[l][k].t()])
                    fw.op(dve, V.tensor_tensor, gm[l][k].ap, gm[l][k].ap,
                          cst(gname).unsqueeze(2).broadcast_to([128, 8, 2]), ALU.mult,
                          reads=[gm[l][k].t(), consts.t()], writes=[gm[l][k].t()])
                    yield
                bgstate["done"] += 1
                for _ in range(spread):
                    yield

        drain(mod_gen(0, [0, 1, 2, 3]))
        bgstate["done"] = 0

        def mod_rest():
            yield from mod_gen(0, [4, 5, 6, 7, 8, 9, 10, 11], spread=MOD_SPREAD)
            yield from mod_gen(1, list(range(12)), spread=MOD_SPREAD)
        bgstate["gen"] = mod_rest()

        def pump(k=1):
            for _ in range(k):
                if bgstate["gen"] is None:
                    return
                try:
                    next(bgstate["gen"])
                except StopIteration:
                    bgstate["gen"] = None

        def ensure_mod(nblocks):
            while bgstate["gen"] is not None and bgstate["done"] < nblocks:
                pump()

        def mod_sh(l, k, c, j):
            o = 0 if k == 0 else 24
            return mod[l].ap[:, o + c, j:j + 1]

        def mod_gate(l, k, c, j):
            o = 16 if k == 0 else 40
            return mod[l].ap[:, o + c, j:j + 1]

        def norm_mod(xsrc, xt, T, l, k, j, hdst, hdst_t, mask=None, plain_gain=None, out_f32=None):
            sq = ar.alloc(uniq("nm_sq"), [8, T], BF16)
            rs = ar.alloc(uniq("nm_rs"), [T], F32)
            tmp = [ar.alloc(uniq("nm_tmp"), [T], F32) for i in range(2)]
            for g in range(2):
                fw.op(act, A.activation, sq.ap[:, g * 4:g * 4 + 4, :], xsrc[:, g * 4:g * 4 + 4, :], AF.Square,
                      reads=[xt], writes=[sq.t(g)])
                yield
            for (a, b) in nchunks(T):
                bk, bt = bank("s")
                fw.mm_group(bk[:, 0:b - a], [(onesb.ap, sq.ap[:, c, a:b]) for c in range(8)],
                            reads=[onesb.t(), sq.t(0), sq.t(1)], writes=[bt])
                yield
                fw.op(act, A.activation, rs.ap[:, a:b], bk[:, 0:b - a], AF.Ln, bias=epsb.ap[:, 0:1], scale=1.0 / 1024,
                      reads=[bt, epsb.t()], writes=[rs.t()])
                yield
            fw.op(act, A.activation, rs.ap, rs.ap, AF.Exp, scale=-0.5, reads=[rs.t()], writes=[rs.t()])
            yield
            for c in range(8):
                tb = tmp[c % 2]
                fw.op(dve, V.tensor_tensor, tb.ap, xsrc[:, c, :], rs.ap, ALU.mult,
                      reads=[xt, rs.t()], writes=[tb.t()])
                yield
                if plain_gain is not None:
                    fw.op(act, A.activation, hdst[:, c, :], tb.ap, AF.Identity, scale=plain_gain[:, c:c + 1],
                          reads=[tb.t(), consts.t()], writes=[hdst_t])
                    yield
                else:
                    fw.op(act, A.activation, hdst[:, c, :], tb.ap, AF.Identity,
                          bias=mod_sh(l, k, c, j), scale=gm[l][k].ap[:, c, j:j + 1],
                          reads=[tb.t(), mod[l].t(0 if k == 0 else 3), gm[l][k].t()], writes=[hdst_t])
                    yield
                    if mask is not None:
                        fw.op(dve, V.tensor_tensor, hdst[:, c, :], hdst[:, c, :], mask, ALU.mult,
                              reads=[hdst_t, vmbox["v"].t()], writes=[hdst_t])
                        yield
            sq.free()
            rs.free()
            for b in tmp:
                b.free()

        vmbox = {}

        def load_w(name, d_w, kchunks, ncols, rows_per_chunk=128):
            wb = ar.alloc(name, [kchunks, ncols], BF16)
            for kc in range(kchunks):
                fw.dma(pool, wb.ap[:rows_per_chunk, kc, :], d_w[kc * rows_per_chunk:(kc + 1) * rows_per_chunk, :],
                       writes=[wb.t()])
            return wb

        def rstd_from_ss(ss_ap, n, width, inv_n, ss_t):
            fw.op(act, A.activation, ss_ap, ss_ap, AF.Ln, bias=epsb.ap[:n, 0:1], scale=inv_n, reads=[ss_t, epsb.t()], writes=[ss_t])
            fw.op(act, A.activation, ss_ap, ss_ap, AF.Exp, scale=-0.5, reads=[ss_t], writes=[ss_t])

        def rope_tm(x_ap, x_t, n, H, D, tab_ap, tab_t, scr_, view3=None, dst3=None, dst_t=None, on_dve=False):
            h2 = D // 2
            if view3 is None:
                view3 = x_ap.rearrange("p (h d) -> p h d", h=H)
            xv = view3.rearrange("p h (i two) -> p h i two", two=2)
            x1 = xv[:, :, :, 0]
            x2 = xv[:, :, :, 1]
            if dst3 is None:
                o1, o2, o_t = x1, x2, x_t
            else:
                ov = dst3.rearrange("p h (i two) -> p h i two", two=2)
                o1, o2, o_t = ov[:, :, :, 0], ov[:, :, :, 1], dst_t
            cs = tab_ap[:, 0:h2].unsqueeze(1).broadcast_to([n, H, h2])
            sn = tab_ap[:, h2:D].unsqueeze(1).broadcast_to([n, H, h2])
            s = [scr_[i].ap[:n, 0:H * h2].rearrange("p (h i) -> p h i", h=H) for i in range(4)]
            ts = [scr_[i].t() for i in range(4)]
            e_, E_ = (dve, V) if on_dve else (pool, G)
            fw.op(e_, E_.tensor_tensor, s[0], x1, cs, ALU.mult, reads=[x_t, tab_t], writes=[ts[0]])
            yield
            fw.op(e_, E_.tensor_tensor, s[1], x2, sn, ALU.mult, reads=[x_t, tab_t], writes=[ts[1]])
            yield
            fw.op(e_, E_.tensor_tensor, s[2], x1, sn, ALU.mult, reads=[x_t, tab_t], writes=[ts[2]])
            yield
            fw.op(e_, E_.tensor_tensor, s[3], x2, cs, ALU.mult, reads=[x_t, tab_t], writes=[ts[3]])
            yield
            fw.op(e_, E_.tensor_tensor, o1, s[0], s[1], ALU.subtract, reads=[ts[0], ts[1]], writes=[o_t])
            yield
            fw.op(e_, E_.tensor_tensor, o2, s[2], s[3], ALU.add, reads=[ts[2], ts[3]], writes=[o_t])
            yield

        def transpose_to(dst_ap_fn, dst_t, src_bf, src_t, n, nblk, blkw=128, group_fn=None):
            done = 0
            while done < nblk:
                g = min(4, nblk - done)
                bk, bt = bank("s")
                bkb = bk[:, :].bitcast(BF16)
                for i in range(g):
                    c = done + i
                    fw.op(pe, nc.tensor.transpose, bkb[:blkw, i * 128:i * 128 + n], src_bf[:n, c * blkw:(c + 1) * blkw],
                          identb.ap[:n, :n], reads=[src_t, identb.t()], writes=[bt])
                    yield
                if group_fn is not None:
                    for (dap, r0_, r1_) in group_fn(done, g):
                        srcv = bkb[r0_:r1_, 0:g * 128].rearrange("p (c m) -> p c m", c=g)[:, :, 0:n]
                        evac_copy(dap, srcv, [bt], [dst_t])
                        yield
                    done += g
                    continue
                for i in range(g):
                    c = done + i
                    d = dst_ap_fn(c)
                    if isinstance(d, list):
                        for (dap, r0_, r1_) in d:
                            evac_copy(dap, bkb[r0_:r1_, i * 128:i * 128 + n], [bt], [dst_t])
                            yield
                    else:
                        evac_copy(d, bkb[:blkw, i * 128:i * 128 + n], [bt], [dst_t])
                        yield
                done += g

        ropeA_ext = ar.alloc("ropeA_ext", [7, 64], F32)
        ropeB_ext = ar.alloc("ropeB_ext", [7, 32], F32)
        fw.dma(sp, ropeA_ext.ap, d_ropeA_ext, writes=[ropeA_ext.t()])
        fw.dma(sp, ropeB_ext.ap, d_ropeB_ext, writes=[ropeB_ext.t()])

        PTN = 6
        BURST = 2
        ptring = []
        ptstate = {"i": 0}
        rlring = []
        rlstate = {"i": 0}
        esink = ar.alloc("esink", [16], F32)
        fw.op(act, A.activation, esink.ap, cst("sink"), AF.Exp, reads=[consts.t()], writes=[esink.t()])

        att_queue = []
        att_pres = []

        def att_pre(fn, at_burst=0):
            att_pres.append((at_burst, fn))

        def attend(nq, s_pairs_fn, key_tiles, v_fn, scale, dst_ap, dst_t, reads_q, mask_fn=None, sink_col=None, act_recip=True):
            pres = {}
            for (bi, fn) in att_pres:
                pres.setdefault(bi, []).append(fn)
            del att_pres[:]
            att_queue.append(dict(nq=nq, s_pairs_fn=s_pairs_fn, key_tiles=key_tiles, v_fn=v_fn, scale=scale, dst_ap=dst_ap,
                                  dst_t=dst_t, reads_q=reads_q, mask_fn=mask_fn, sink_col=sink_col, act_recip=act_recip,
                                  pres=pres, acc=None))

        def _att_s(p, bi):
            nq = p["nq"]
            groups = []
            items = []
            for ki in p["bursts"][bi]:
                nk = p["key_tiles"][ki]
                sb_, sbt = bank("s")
                pairs, rtr = p["s_pairs_fn"](ki)
                np_ = len(pairs)
                groups.append((sb_[:nk, 0:nq], [(l, r, i == 0, i == np_ - 1) for i, (l, r) in enumerate(pairs)],
                               rtr + p["reads_q"], [sbt]))
                items.append((ki, nk, sb_, sbt))
            fw.mm_burst(groups)
            cur = []
            for (ki, nk, sb_, sbt) in items:
                pt = ptring[ptstate["i"] % PTN]
                ptstate["i"] += 1
                fw.op(act, A.activation, pt.ap[:nk, 0:nq], sb_[:nk, 0:nq], AF.Exp, scale=p["scale"],
                      reads=[sbt], writes=[pt.t()])
                if p["mask_fn"] is not None:
                    m = p["mask_fn"](ki)
                    if m is not None:
                        map_, mt = m
                        fw.op(dve, V.tensor_tensor, pt.ap[:nk, 0:nq], pt.ap[:nk, 0:nq], map_, ALU.mult,
                              reads=[pt.t(), mt], writes=[pt.t()])
                cur.append((ki, nk, pt))
            return cur

        def _att_pv(p, bi, cur):
            nq = p["nq"]
            nkt = len(p["key_tiles"])
            if p["acc"] is None:
                p["acc"] = bank("a")
            acc, acct = p["acc"]
            groups = []
            for (pki, pnk, ppt) in cur:
                vap, vtr = p["v_fn"](pki)
                groups.append((acc[:, 0:nq], [(vap, ppt.ap[:pnk, 0:nq], pki == 0, pki == nkt - 1)],
                               vtr + [ppt.t()], [acct]))
            fw.mm_burst(groups)
            if bi == len(p["bursts"]) - 1:
                _att_fin(p)

        def flush_attends():
            plans = att_queue[:]
            del att_queue[:]
            for i in range(PTN):
                ptring.append(ar.alloc(uniq("pt"), [512], BF16))
            rlring.append(ar.alloc(uniq("rlb"), [512], F32))
            LAG = 2
            flat = []
            for p in plans:
                nkt = len(p["key_tiles"])
                p["bursts"] = [list(range(i, min(nkt, i + BURST))) for i in range(0, nkt, BURST)]
                nb = len(p["bursts"])
                pres = {}
                for k_, fns in p["pres"].items():
                    pres.setdefault(min(k_, nb - 1), []).extend(fns)
                p["pres"] = pres
                for bi in range(nb):
                    flat.append((p, bi))
            pendq = []
            for (p, bi) in flat:
                pump()
                for fn in p["pres"].get(bi, []):
                    fn()
                cur = _att_s(p, bi)
                pendq.append((p, bi, cur))
                if len(pendq) > LAG:
                    _att_pv(*pendq.pop(0))
            while pendq:
                _att_pv(*pendq.pop(0))
            for b_ in ptring + rlring:
                b_.free()
            del ptring[:]
            del rlring[:]

        def _att_fin(p):
            nq, sink_col, act_recip = p["nq"], p["sink_col"], p["act_recip"]
            dst_ap, dst_t = p["dst_ap"], p["dst_t"]
            acc, acct = p["acc"]
            rlb = rlring[0]
            rlstate["i"] += 1
            if act_recip:
                if sink_col is not None:
                    fw.op(act, A.activation, rlb.ap[64:128, 0:nq], acc[64:128, 0:nq], AF.Ln,
                          bias=esink.ap[64:128, sink_col:sink_col + 1], reads=[acct, esink.t()], writes=[rlb.t()])
                else:
                    fw.op(act, A.activation, rlb.ap[64:128, 0:nq], acc[64:128, 0:nq], AF.Ln, reads=[acct], writes=[rlb.t()])
                fw.op(act, A.activation, rlb.ap[64:128, 0:nq], rlb.ap[64:128, 0:nq], AF.Exp, scale=-1.0,
                      reads=[rlb.t()], writes=[rlb.t()])
            elif sink_col is not None:
                fw.op(dve, V.tensor_scalar_add, rlb.ap[64:128, 0:nq], acc[64:128, 0:nq], esink.ap[64:128, sink_col:sink_col + 1],
                      reads=[acct, esink.t()], writes=[rlb.t()])
                fw.op(dve, V.reciprocal, rlb.ap[64:128, 0:nq], rlb.ap[64:128, 0:nq], reads=[rlb.t()], writes=[rlb.t()])
            else:
                fw.op(dve, V.reciprocal, rlb.ap[64:128, 0:nq], acc[64:128, 0:nq], reads=[acct], writes=[rlb.t()])
            fw.op(dve, V.tensor_tensor, dst_ap, acc[0:64, 0:nq], rlb.ap[64:128, 0:nq], ALU.mult,
                  reads=[acct, rlb.t()], writes=[dst_t])

        ffn_pref = {}

        def ffn_prefetch(l):
            wring = [ar.alloc("wup%d" % i, [8, 256], BF16) for i in range(4)]
            for jj in range(3):
                fw.dma(pool, wring[jj % 4].ap, d_wup[l, jj], writes=[wring[jj % 4].t()])
            ffn_pref["wring"] = wring

        def ffn(l, groups):
            Ttot = sum(g["T"] for g in groups)
            actT = ar.alloc("actT", [NJ, Ttot], BF16)
            ctmp = [ar.alloc("ctmp%d" % i, [512], F32) for i in range(2)]
            cst_i = {"i": 0}
            wring = ffn_pref.pop("wring")
            cw = cst("convw%d" % l).rearrange("p (j k) -> p j k", k=3)
            cb = cst("convb%d" % l)
            work = []
            toff = 0
            for g in groups:
                g["toff"] = toff
                for (sa, sb2) in g["segs"]:
                    for (a, b) in nchunks(sb2 - sa, 510):
                        work.append((g, sa + a, sa + b, sa, sb2))
                toff += g["T"]

            def issue_w(jj):
                wb = wring[jj % 4]
                fw.dma(pool, wb.ap, d_wup[l, jj], writes=[wb.t()])

            dring = [ar.alloc("wdn%d" % i, [NJ, 128], BF16) for i in range(2)]

            def issue_d(m):
                fw.dma(pool, dring[m % 2].ap, d_wdn[l, m], writes=[dring[m % 2].t()])

            for jj in range(NJ):
                if jj == NJ - 4:
                    issue_d(0)
                if jj + 3 < NJ:
                    issue_w(jj + 3)
                wb = wring[jj % 4]
                for (g, a, b, sa, sb2) in work:
                    ga = max(a - 1, sa)
                    gb = min(b + 1, sb2)
                    ng = gb - ga
                    n = b - a
                    gk, gt = bank("s")
                    vk, vt = bank("s")
                    fw.mm_group(gk[:, 0:ng], [(wb.ap[:, kc, 0:128], g["hT"][:, kc, ga:gb]) for kc in range(8)],
                                reads=[wb.t(), g["ht"]], writes=[gt])
                    fw.mm_group(vk[:, 0:n], [(wb.ap[:, kc, 128:256], g["hT"][:, kc, a:b]) for kc in range(8)],
                                reads=[wb.t(), g["ht"]], writes=[vt])
                    ct = ctmp[cst_i["i"] % 2]
                    cst_i["i"] += 1
                    o = a - ga
                    fw.op(act, A.activation, ct.ap[:, 0:n], gk[:, o:o + n], AF.Identity, bias=cb[:, jj:jj + 1],
                          scale=cw[:, jj, 1:2], reads=[gt, consts.t()], writes=[ct.t()])
                    lo = 0 if o == 1 else 1
                    if n - lo > 0:
                        fw.op(dve, V.scalar_tensor_tensor, ct.ap[:, lo:n], gk[:, o + lo - 1:o + n - 1], cw[:, jj, 0:1],
                              ct.ap[:, lo:n], ALU.mult, ALU.add, reads=[gt, ct.t(), consts.t()], writes=[ct.t()])
                    hi = n if gb > b else n - 1
                    if hi > 0:
                        fw.op(dve, V.scalar_tensor_tensor, ct.ap[:, 0:hi], gk[:, o + 1:o + 1 + hi], cw[:, jj, 2:3],
                              ct.ap[:, 0:hi], ALU.mult, ALU.add, reads=[gt, ct.t(), consts.t()], writes=[ct.t()])
                    fw.op(act, A.activation, ct.ap[:, 0:n], ct.ap[:, 0:n], AF.Silu, reads=[ct.t()], writes=[ct.t()])
                    fw.op(dve, V.tensor_tensor, actT.ap[:, jj, g["toff"] + a:g["toff"] + b], ct.ap[:, 0:n], vk[:, 0:n],
                          ALU.mult, reads=[ct.t(), vt], writes=[actT.t(jj)])
            for b_ in wring:
                b_.free()
            for b_ in ctmp:
                b_.free()
            allact = [actT.t(jj) for jj in range(NJ)]
            for m in range(8):
                if m + 1 < 8:
                    issue_d(m + 1)
                wd = dring[m % 2]
                for g in groups:
                    for (a, b) in nchunks(g["T"]):
                        bk, bt = bank("s")
                        fw.mm_group(bk[:, 0:b - a],
                                    [(wd.ap[:, jj, :], actT.ap[:, jj, g["toff"] + a:g["toff"] + b]) for jj in range(NJ)],
                                    reads=[wd.t()] + allact, writes=[bt])
                        fw.op(dve, V.scalar_tensor_tensor, g["xT"][:, m, a:b], bk[:, 0:b - a], mod_gate(l, 1, m, g["j"]),
                              g["xT"][:, m, a:b], ALU.mult, ALU.add, reads=[bt, mod[l].t(5), g["xt"]], writes=[g["xt"]])
            for b_ in dring:
                b_.free()
            actT.free()

        def out_proj(l, wout, OT, groups):
            allo = [OT.t(c) for c in range(8)]
            for m in range(8):
                for g in groups:
                    for (a, b) in nchunks(g["T"]):
                        bk, bt = bank("s")
                        fw.mm_group(bk[:, 0:b - a],
                                    [(wout.ap[:, kc, m * 128:(m + 1) * 128], OT.ap[:, kc, g["ooff"] + a:g["ooff"] + b])
                                     for kc in range(8)],
                                    reads=[wout.t()] + allo, writes=[bt])
                        fw.op(dve, V.scalar_tensor_tensor, g["xT"][:, m, a:b], bk[:, 0:b - a], mod_gate(l, 0, m, g["j"]),
                              g["xT"][:, m, a:b], ALU.mult, ALU.add, reads=[bt, mod[l].t(2), g["xt"]], writes=[g["xt"]])

        hTp = ar.alloc("hTp", [8, TP], BF16)
        hTs = ar.alloc("hTs", [8, TS], BF16)
        run_lanes([norm_mod(xTp.ap, xTp.t(), TP, 0, 0, 0, hTp.ap, hTp.t()),
                   norm_mod(xTs.ap, xTs.t(), TS, 0, 0, 1, hTs.ap, hTs.t())])

        wine = load_w("wine", d_wine, 8, 1440)
        wuq = load_w("wuq", d_wuq, 3, 768)
        wukv = load_w("wukv", d_wukv, 2, 1024)

        def proj_tm(hT_ap, h_t, t0, n, w, c0, c1):
            bk, bt = bank("s")
            fw.mm_group(bk[:n, 0:c1 - c0], [(hT_ap[:, kc, t0:t0 + n], w.ap[:, kc, c0:c1]) for kc in range(8)],
                        reads=[h_t, w.t()], writes=[bt])
            return bk, bt

        def head_norm(bk, bt, n, H, gname, dst_f32_ap, dst_t):
            w = H * 64
            fw.op(act, A.activation, S.tmA.ap[:n, 0:w], bk[:n, 0:w], AF.Square, reads=[bt], writes=[S.tmA.t()])
            yield
            fw.op(dve, V.tensor_reduce, S.ssb.ap[:n, 0:H], S.tmA.ap[:n, 0:w].rearrange("p (h d) -> p h d", h=H), AX.X, ALU.add,
                  reads=[S.tmA.t()], writes=[S.ssb.t()])
            yield
            rstd_from_ss(S.ssb.ap[:n, 0:H], n, H, 1.0 / 64, S.ssb.t())
            yield
            fw.op(dve, V.tensor_tensor, S.tmA.ap[:n, 0:w].rearrange("p (h d) -> p h d", h=H),
                  bk[:n, 0:w].rearrange("p (h d) -> p h d", h=H),
                  S.ssb.ap[:n, 0:H].unsqueeze(2).broadcast_to([n, H, 64]), ALU.mult,
                  reads=[bt, S.ssb.t()], writes=[S.tmA.t()])
            yield
            fw.op(dve, V.tensor_tensor, dst_f32_ap.rearrange("p (h d) -> p h d", h=H),
                  S.tmA.ap[:n, 0:w].rearrange("p (h d) -> p h d", h=H),
                  cst(gname)[:n, :].unsqueeze(1).broadcast_to([n, H, 64]), ALU.mult,
                  reads=[S.tmA.t(), consts.t()], writes=[dst_t])
            yield

        def row_norm(bk, bt, n, c0, width, gname, dst_ap, dst_t):
            fw.op(dve, V.memset, S.ssb.ap[:n, 8:9], 0.0, writes=[S.ssb.t()])
            yield
            fw.op(act, A.activation, S.tmA.ap[:n, 0:width], bk[:n, c0:c0 + width], AF.Square, accum_out=S.ssb.ap[:n, 8:9],
                  reads=[bt], writes=[S.tmA.t(), S.ssb.t()])
            yield
            rstd_from_ss(S.ssb.ap[:n, 8:9], n, 1, 1.0 / width, S.ssb.t())
            yield
            fw.op(dve, V.scalar_tensor_tensor, dst_ap, bk[:n, c0:c0 + width], S.ssb.ap[:n, 8:9], cst(gname)[:n, :],
                  ALU.mult, ALU.mult, reads=[bt, S.ssb.t(), consts.t()], writes=[dst_t])
            yield

        def even_qa(hT_ap, h_t, t0, n, QaT, tok_off, ropeA=None):
            bk, bt = proj_tm(hT_ap, h_t, t0, n, wine, 0, 512)
            yield
            yield from head_norm(bk, bt, n, 8, "gq", S.tmB.ap[:n, 0:512], S.tmB.t())
            if ropeA is not None:
                yield from rope_tm(S.tmB.ap[:n, 0:512], S.tmB.t(), n, 8, 64, ropeA[0], ropeA[1], S.scr,
                                   dst3=S.tmbf.ap[:n, 0:512].rearrange("p (h d) -> p h d", h=8), dst_t=S.tmbf.t())
            else:
                fw.op(act, A.copy, S.tmbf.ap[:n, 0:512], S.tmB.ap[:n, 0:512], reads=[S.tmB.t()], writes=[S.tmbf.t()])
                yield
            yield from transpose_to(None, QaT.t(tok_off // 128), S.tmbf.ap, S.tmbf.t(), n, 4,
                                    group_fn=lambda c0, g: [(QaT.ap[0:64, 2 * c0:2 * (c0 + g):2, tok_off:tok_off + n], 0, 64),
                                                            (QaT.ap[64:128, 2 * c0 + 1:2 * (c0 + g):2, tok_off:tok_off + n], 64, 128)])

        def even_cq(hT_ap, h_t, t0, n, cqnT, tok_off):
            bk, bt = proj_tm(hT_ap, h_t, t0, n, wine, 768, 1152)
            yield
            yield from row_norm(bk, bt, n, 0, 384, "gcq", S.tmbf.ap[:n, 0:384], S.tmbf.t())
            yield from transpose_to(None, cqnT.t(tok_off // 128), S.tmbf.ap, S.tmbf.t(), n, 3,
                                    group_fn=lambda c0, g: [(cqnT.ap[:, c0:c0 + g, tok_off:tok_off + n], 0, 128)])

        def even_qb(cqnT, t0, n, QbT, ropeB=None):
            for half in range(2):
                bk, bt = bank("s")
                fw.mm_group(bk[:n, 0:384], [(cqnT.ap[:, kc, t0:t0 + n], wuq.ap[:, kc, half * 384:(half + 1) * 384]) for kc in range(3)],
                            reads=[cqnT.t(t0 // 128), wuq.t()], writes=[bt])
                yield
                if ropeB is not None:
                    fw.op(act, A.copy, S.tmB.ap[:n, half * 384:(half + 1) * 384], bk[:n, 0:384], reads=[bt], writes=[S.tmB.t()])
                    yield
                else:
                    fw.op(act, A.copy, S.tmbf.ap[:n, half * 384:(half + 1) * 384], bk[:n, 0:384], reads=[bt], writes=[S.tmbf.t()])
                    yield
            if ropeB is not None:
                v3 = S.tmB.ap[:n, 0:768].rearrange("p (h d) -> p h d", h=8)[:, :, 64:96]
                yield from rope_tm(None, S.tmB.t(), n, 8, 32, ropeB[0], ropeB[1], S.scr, view3=v3)
                fw.op(act, A.copy, S.tmbf.ap[:n, 0:768], S.tmB.ap[:n, 0:768], reads=[S.tmB.t()], writes=[S.tmbf.t()])
                yield
            yield from transpose_to(None, QbT.t(t0 // 128), S.tmbf.ap, S.tmbf.t(), n, 8, blkw=96,
                                    group_fn=lambda c0, g: [(QbT.ap[0:96, c0:c0 + g, t0:t0 + n], 0, 96)])

        def even_kv_side(src, n, KaT2, Va, ckvnT, kpeT2, key_off, vtile, outs=None, ropeA=None, ropeB=None):
            if src[0] == "proj":
                _, hT_ap, h_t, t0, wkv, ckv0, cc0 = src
                bk, bt = proj_tm(hT_ap, h_t, t0, n, wkv, ckv0, ckv0 + 256)
                yield
                kf = S.out_stage()
                yield from head_norm(bk, bt, n, 2, "gk", kf.ap[:n, 0:128], kf.t())
                fw.op(act, A.copy, kf.ap[:n, 128:256], bk[:n, 128:256], reads=[bt], writes=[kf.t()])
                yield
                bk2, bt2 = proj_tm(hT_ap, h_t, t0, n, wkv, cc0, cc0 + 288)
                yield
                yield from row_norm(bk2, bt2, n, 0, 256, "gckv", kf.ap[:n, 256:512], kf.t())
                fw.op(act, A.copy, kf.ap[:n, 512:544], bk2[:n, 256:288], reads=[bt2], writes=[kf.t()])
                yield
                if outs is not None:
                    r0 = outs
                    fw.dma(sp, o_ak[r0:r0 + n, :], kf.ap[:n, 0:128], reads=[kf.t()], is_output=True)
                    yield
                    fw.dma(sp, o_av[r0:r0 + n, :], kf.ap[:n, 128:256], reads=[kf.t()], is_output=True)
                    yield
                    fw.dma(sp, o_ckv[r0:r0 + n, :], kf.ap[:n, 256:512], reads=[kf.t()], is_output=True)
                    yield
                    fw.dma(sp, o_kpe[r0:r0 + n, :], kf.ap[:n, 512:544], reads=[kf.t()], is_output=True)
                    yield
                kfa, kft = kf.ap, kf.t()
                if ropeA is not None:
                    yield from rope_tm(kfa[:n, 0:128], kft, n, 2, 64, ropeA[0], ropeA[1], S.scr)
                    yield from rope_tm(kfa[:n, 512:544], kft, n, 1, 32, ropeB[0], ropeB[1], S.scr)
            else:
                _, kf = src
                kfa, kft = kf.ap, kf.t()
            fw.op(dve, V.tensor_copy, S.tmbf.ap[:n, 0:256].rearrange("p (k r d) -> p k r d", k=2, r=2),
                  kfa[:n, 0:128].rearrange("p (k d) -> p k d", k=2).unsqueeze(2).broadcast_to([n, 2, 2, 64]),
                  reads=[kft], writes=[S.tmbf.t()])
            yield
            yield from transpose_to(None, KaT2.t(key_off // 128), S.tmbf.ap, S.tmbf.t(), n, 2,
                                    group_fn=lambda c0, g: [(KaT2.ap[:, c0:c0 + g, key_off:key_off + n], 0, 128)])
            fw.op(act, A.copy, Va.ap[:n, vtile, :, 0:64], kfa[:n, 128:256].rearrange("p (k d) -> p k d", k=2),
                  reads=[kft], writes=[Va.t(vtile)])
            yield
            fw.op(dve, V.tensor_copy, S.tmbf.ap[:n, 256:512], kfa[:n, 256:512], reads=[kft], writes=[S.tmbf.t()])
            yield
            yield from transpose_to(None, ckvnT.t(key_off // 128), S.tmbf.ap[:, 256:512], S.tmbf.t(), n, 2,
                                    group_fn=lambda c0, g: [(ckvnT.ap[:, c0:c0 + g, key_off:key_off + n], 0, 128)])
            fw.op(dve, V.tensor_copy, S.tmbf.ap[:n, 576:608], kfa[:n, 512:544], reads=[kft], writes=[S.tmbf.t()])
            yield
            yield from transpose_to(lambda c: [(kpeT2.ap[64:96, key_off:key_off + n], 64, 96)], kpeT2.t(key_off // 128),
                         S.tmbf.ap[:, 512:608], S.tmbf.t(), n, 1, blkw=96)

        def ones_cols(Vbuf, ntile, nh):
            fw.op(dve, V.memset, Vbuf.ap[:, :, :, 64:128], 1.0, writes=alltk(Vbuf, ntile))

        def even_attention(T, QaT, QbT, KaT2, Va, ckvnT, kpeT2, NK, OT, ooff, qsegs, act_recip=True):
            SC_A = 64 ** -0.5
            SC_B = 96 ** -0.5
            for h in range(8):
                kvh = h // 4
                pb = (h % 2) * 64
                for (q0, q1, k0, k1) in qsegs:
                    kts = ttiles(k1 - k0)
                    for (a, b) in nchunks(q1 - q0):
                        qa, qb = q0 + a, q0 + b
                        attend(qb - qa,
                               lambda ki, kts=kts, k0=k0, kvh=kvh, h=h, qa=qa, qb=qb: (
                                   [(KaT2.ap[:, kvh, k0 + kts[ki][0]:k0 + kts[ki][0] + kts[ki][1]],
                                     QaT.ap[:, h, qa:qb])], [KaT2.t((k0 + kts[ki][0]) // 128)]),
                               [kt[1] for kt in kts],
                               lambda ki, kts=kts, k0=k0, kvh=kvh: (
                                   Va.ap[:kts[ki][1], (k0 + kts[ki][0]) // 128, kvh, :], [Va.t((k0 + kts[ki][0]) // 128)]),
                               SC_A, OT.ap[pb:pb + 64, h // 2, ooff + qa:ooff + qb], OT.t(h // 2), alltk(QaT, 8), act_recip=act_recip)
            flush_attends()
            KbT = [ar.alloc("KbT%d" % i, [NK], BF16) for i in range(2)]
            nvt = cdiv(NK, 128)
            Vb = [ar.alloc("Vb%d" % i, [nvt, 2, 128], BF16) for i in range(2)]
            for i in range(2):
                ones_cols(Vb[i], nvt, 2)
                fw.op(dve, V.memset, KbT[i].ap[96:128, :], 0.0, writes=[KbT[i].t()])
            def build_V(c):
                vb = Vb[c % 2]
                for vt_, (k0_, nk_) in enumerate(ttiles(NK)):
                    bk, bt = bank("s")
                    fw.mm_group(bk[:nk_, 0:128],
                                [(ckvnT.ap[:, kc, k0_:k0_ + nk_], wukv.ap[:, kc, 512 + c * 128:512 + (c + 1) * 128]) for kc in range(2)],
                                reads=[wukv.t(), ckvnT.t(k0_ // 128)], writes=[bt])
                    evac_copy(vb.ap[:nk_, vt_, :, 0:64], bk[:nk_, 0:128].rearrange("p (k d) -> p k d", k=2), [bt], [vb.t()],
                              force_dve=(NK > 1024))

            def build_K(h):
                kb = KbT[h % 2]
                for (a, b) in nchunks(NK):
                    bk, bt = bank("s")
                    fw.mm_group(bk[0:64, 0:b - a], [(wukv.ap[:, kc, h * 64:(h + 1) * 64], ckvnT.ap[:, kc, a:b]) for kc in range(2)],
                                reads=[wukv.t()] + [ckvnT.t(i_) for i_ in range(a // 128, cdiv(b, 128))], writes=[bt])
                    evac_copy(kb.ap[0:64, a:b], bk[0:64, 0:b - a], [bt], [kb.t()], force_dve=(NK > 1024))
                fw.op(dve, V.tensor_copy, kb.ap[64:96, :], kpeT2.ap[64:96, 0:NK], reads=alltk(kpeT2, cdiv(NK, 128)), writes=[kb.t()])

            build_V(0)
            build_K(0)
            for h in range(8):
                c, hh = h // 2, h % 2
                if h + 1 < 8:
                    att_pre(lambda h=h: build_K(h + 1), 0)
                    if (h + 1) % 2 == 0:
                        att_pre(lambda c=c: build_V(c + 1), 3)
                vb = Vb[c % 2]
                kb = KbT[h % 2]
                pb = hh * 64
                for (q0, q1, k0, k1) in qsegs:
                    kts = ttiles(k1 - k0)
                    for (a, b) in nchunks(q1 - q0):
                        qa, qb = q0 + a, q0 + b
                        attend(qb - qa,
                               lambda ki, kts=kts, k0=k0, h=h, qa=qa, qb=qb, kb=kb: (
                                   [(kb.ap[:, k0 + kts[ki][0]:k0 + kts[ki][0] + kts[ki][1]],
                                     QbT.ap[:, h, qa:qb])], [kb.t()]),
                               [kt[1] for kt in kts],
                               lambda ki, kts=kts, k0=k0, hh=hh, vb=vb: (
                                   vb.ap[:kts[ki][1], (k0 + kts[ki][0]) // 128, hh, :], [vb.t()]),
                               SC_B, OT.ap[pb:pb + 64, 4 + c, ooff + qa:ooff + qb], OT.t(4 + c),
                               alltk(QbT, 8), act_recip=act_recip)
            flush_attends()
            for b_ in KbT + Vb:
                b_.free()

        OT = ar.alloc("OTp", [8, TP], BF16)

        QaT = ar.alloc("QaTp", [8, TP], BF16)
        cqnT = ar.alloc("cqnTp", [3, TP], BF16)
        QbT = ar.alloc("QbTp", [8, TP], BF16)
        fw.op(dve, V.memset, QaT.ap, 0.0, writes=alltk(QaT, 8))
        fw.op(dve, V.memset, QbT.ap[96:128, :, :], 0.0, writes=alltk(QbT, 8))
        KaT2 = ar.alloc("KaT2p", [2, TP], BF16)
        Va = ar.alloc("Vap", [4, 2, 128], BF16)
        ckvnT = ar.alloc("ckvnTp", [2, TP], BF16)
        kpeT2 = ar.alloc("kpeT2p", [TP], BF16)
        ones_cols(Va, 4, 2)
        def chain_p0a(ti, t0, n, QaT=QaT):
            yield from even_qa(hTp.ap, hTp.t(), t0, n, QaT, t0)

        def chain_p0b(ti, t0, n, cqnT=cqnT, QbT=QbT):
            yield from even_cq(hTp.ap, hTp.t(), t0, n, cqnT, t0)
            yield from even_qb(cqnT, t0, n, QbT)

        def chain_p0c(ti, t0, n, KaT2=KaT2, Va=Va, ckvnT=ckvnT, kpeT2=kpeT2):
            yield from even_kv_side(("proj", hTp.ap, hTp.t(), t0, wine, 512, 1152), n, KaT2, Va, ckvnT, kpeT2, t0, ti, outs=t0)
        gl_ = []
        for ti, (t0, n) in enumerate(ttiles(TP)):
            gl_ += [chain_p0b(ti, t0, n), chain_p0a(ti, t0, n), chain_p0c(ti, t0, n)]
        run_lanes(gl_, background=True)
        even_attention(TP, QaT, QbT, KaT2, Va, ckvnT, kpeT2, TP, OT, 0,
                       [(0, SEQ, 0, SEQ), (SEQ, 2 * SEQ, SEQ, 2 * SEQ)])
        for b_ in (QaT, cqnT, QbT, KaT2, Va, ckvnT, kpeT2):
            b_.free()

        hTp.free()
        wukv.free()
        ensure_mod(2)
        gp = dict(T=TP, ooff=0, xT=xTp.ap, xt=xTp.t(), j=0)
        woute = load_w("woute", d_woute, 8, 1024)
        out_proj(0, woute, OT, [gp])
        woute.free()
        OT.free()
        ensure_mod(20)
        for b_ in wmring:
            b_.free()
        if STAGE >= 2:
            QaT = ar.alloc("QaTs", [8, TS], BF16)
            QbT = ar.alloc("QbTs", [8, TS], BF16)
            fw.op(dve, V.memset, QaT.ap, 0.0, writes=alltk(QaT, 8))
            fw.op(dve, V.memset, QbT.ap[96:128, :, :], 0.0, writes=alltk(QbT, 8))
            cqnT = ar.alloc("cqnTs", [3, TS], BF16)
            def chain_sqa(ti, t0, n, QaT=QaT):
                yield from even_qa(hTs.ap, hTs.t(), t0, n, QaT, t0, ropeA=(ropeA_ext.ap[:n, ti, :], ropeA_ext.t()))

            def chain_sqb(ti, t0, n, cqnT=cqnT, QbT=QbT):
                yield from even_cq(hTs.ap, hTs.t(), t0, n, cqnT, t0)
                yield from even_qb(cqnT, t0, n, QbT, ropeB=(ropeB_ext.ap[:n, ti, :], ropeB_ext.t()))
            gl_ = []
            for ti, (t0, n) in enumerate(ttiles(TS)):
                gl_ += [chain_sqb(ti, t0, n), chain_sqa(ti, t0, n)]
            run_lanes(gl_, width=4)
            cqnT.free()
            hTs.free()
            wine.free()
            wuq.free()
            ropeA_ext.free()
            ropeB_ext.free()
            winekv = ar.alloc("winekv", [8, 544], BF16)
            for kc in range(8):
                fw.dma(pool, winekv.ap[:, kc, 0:256], d_wine[kc * 128:(kc + 1) * 128, 512:768], writes=[winekv.t()])
                fw.dma(pool, winekv.ap[:, kc, 256:544], d_wine[kc * 128:(kc + 1) * 128, 1152:1440], writes=[winekv.t()])
            KaT2 = ar.alloc("KaT2s", [2, NK0], BF16)
            Va = ar.alloc("Vas", [NK0 // 128, 2, 128], BF16)
            ckvnT = ar.alloc("ckvnTs", [2, NK0], BF16)
            kpeT2 = ar.alloc("kpeT2s", [NK0], BF16)
            ones_cols(Va, NK0 // 128, 2)
            ropeA_full = ar.alloc("ropeA_full", [16, 64], F32)
            ropeB_full = ar.alloc("ropeB_full", [16, 32], F32)
            fw.dma(sp, ropeA_full.ap, d_ropeA_full, writes=[ropeA_full.t()])
            fw.dma(sp, ropeB_full.ap, d_ropeB_full, writes=[ropeB_full.t()])
            FB = 128
            for li_ in (2, 3):
                if not lanes[li_].live:
                    lane_alloc(lanes[li_], li_)
            for L in lanes[:4]:
                L.xb = ar.alloc(uniq("gfx"), [1024], F32)
                L.xn = ar.alloc(uniq("gfn"), [1024], BF16)
                L.hfT = ar.alloc(uniq("hfT"), [8, FB], BF16)

            def chain_gf(blk, KaT2=KaT2, Va=Va, ckvnT=ckvnT, kpeT2=kpeT2):
                xb, xn, hfT = S.xb, S.xn, S.hfT
                fw.dma(sp, xb.ap, d_xf[blk * FB:(blk + 1) * FB, :], writes=[xb.t()])
                yield
                fw.op(dve, V.memset, S.ssb.ap[:, 9:10], 0.0, writes=[S.ssb.t()])
                yield
                fw.op(act, A.activation, xn.ap, xb.ap, AF.Square, accum_out=S.ssb.ap[:, 9:10],
                      reads=[xb.t()], writes=[xn.t(), S.ssb.t()])
                yield
                rstd_from_ss(S.ssb.ap[:, 9:10], FB, 1, 1.0 / 1024, S.ssb.t())
                yield
                fw.op(dve, V.tensor_scalar_mul, xn.ap, xb.ap, S.ssb.ap[:, 9:10], reads=[xb.t(), S.ssb.t()], writes=[xn.t()])
                yield
                bks = []
                for g in range(2):
                    bk, bt = bank("s")
                    bkb = bk[:, :].bitcast(BF16)
                    for c4 in range(4):
                        c = g * 4 + c4
                        fw.op(pe, nc.tensor.transpose, bkb[:, c4 * 128:(c4 + 1) * 128], xn.ap[:, c * 128:(c + 1) * 128],
                              identb.ap, reads=[xn.t(), identb.t()], writes=[bt])
                        yield
                    bks.append((bkb, bt))
                for c in range(8):
                    bkb, bt = bks[c // 4]
                    src = bkb[:, (c % 4) * 128:(c % 4 + 1) * 128]
                    if c % 2 == 0:
                        fw.op(act, A.activation, hfT.ap[:, c, :], src, AF.Identity, bias=mod_sh(0, 0, c, 1),
                              scale=gm[0][0].ap[:, c, 1:2], reads=[bt, mod[0].t(0), gm[0][0].t()], writes=[hfT.t()])
                    else:
                        fw.op(dve, V.tensor_scalar, hfT.ap[:, c, :], src, gm[0][0].ap[:, c, 1:2], mod_sh(0, 0, c, 1),
                              ALU.mult, ALU.add, reads=[bt, mod[0].t(0), gm[0][0].t()], writes=[hfT.t()])
                    yield
                gt_ = blk
                yield from even_kv_side(("proj", hfT.ap, hfT.t(), 0, winekv, 0, 256), FB, KaT2, Va, ckvnT, kpeT2,
                                        blk * FB, gt_,
                                        ropeA=(ropeA_full.ap[:, gt_, :], ropeA_full.t()),
                                        ropeB=(ropeB_full.ap[:, gt_, :], ropeB_full.t()))
            run_lanes([chain_gf(blk) for blk in range(NFULL // FB)], width=4)
            for L in lanes[:4]:
                L.xb.free()
                L.xn.free()
                L.hfT.free()
            lane_free(lanes[2])
            lane_free(lanes[3])
            wukv = load_w("wukv", d_wukv, 2, 1024)
            ropeA_full.free()
            ropeB_full.free()
            winekv.free()
            def chain_ctx0(ti, KaT2=KaT2, Va=Va, ckvnT=ckvnT, kpeT2=kpeT2):
                kf = S.out_stage()
                r0 = ti * 128
                fw.dma(sp, kf.ap[:, 0:128], d_cak[r0:r0 + 128, :], writes=[kf.t()])
                fw.dma(sp, kf.ap[:, 128:256], d_cav[r0:r0 + 128, :], writes=[kf.t()])
                fw.dma(sp, kf.ap[:, 256:512], d_cbckv[r0:r0 + 128, :], writes=[kf.t()])
                fw.dma(sp, kf.ap[:, 512:544], d_cbkpe[r0:r0 + 128, :], writes=[kf.t()])
                yield from even_kv_side(("cache", kf), 128, KaT2, Va, ckvnT, kpeT2, NFULL + r0, (NFULL + r0) // 128)
            run_lanes([chain_ctx0(ti) for ti in range(NCTX // 128)], width=4)
            OT = ar.alloc("OTs", [8, TS], BF16)
            even_attention(TS, QaT, QbT, KaT2, Va, ckvnT, kpeT2, NK0, OT, 0, [(0, TS, 0, NK0)], act_recip=False)
            wukv.free()
            for b_ in (QaT, QbT, KaT2, Va, ckvnT, kpeT2):
                b_.free()
        else:
            hTs.free()
            wine.free()
            wuq.free()

        if STAGE >= 2:
            gs = dict(T=TS, ooff=0, xT=xTs.ap, xt=xTs.t(), j=1)
            woute = load_w("woute", d_woute, 8, 1024)
            out_proj(0, woute, OT, [gs])
            woute.free()
            OT.free()

        ffn_prefetch(0)
        hTp = ar.alloc("h2Tp", [8, TP], BF16)
        ng_ = [norm_mod(xTp.ap, xTp.t(), TP, 0, 1, 0, hTp.ap, hTp.t())]
        fg = [dict(hT=hTp.ap, ht=hTp.t(), T=TP, xT=xTp.ap, xt=xTp.t(), j=0, segs=[(0, SEQ), (SEQ, 2 * SEQ)])]
        if STAGE >= 2:
            hTs = ar.alloc("h2Ts", [8, TS], BF16)
            vmask = ar.alloc("vmask", [TS], F32)
            vmbox["v"] = vmask
            fw.dma(sp, vmask.ap, d_vmask, writes=[vmask.t()])
            ng_.append(norm_mod(xTs.ap, xTs.t(), TS, 0, 1, 1, hTs.ap, hTs.t(), mask=vmask.ap))
        run_lanes(ng_)
        if STAGE >= 2:
            fg.append(dict(hT=hTs.ap, ht=hTs.t(), T=TS, xT=xTs.ap, xt=xTs.t(), j=1, segs=[(0, TS)]))
        ffn(0, fg)
        hTp.free()
        if STAGE >= 2:
            hTs.free()

        wino = load_w("wino", d_wino, 8, 1280)
        hTp = ar.alloc("h1Tp", [8, TP], BF16)
        drain(norm_mod(xTp.ap, xTp.t(), TP, 1, 0, 0, hTp.ap, hTp.t()))
        OT = ar.alloc("OT1", [8, TP + TQ1], BF16)
        SC_C = 64 ** -0.5

        def odd_q(hT_ap, h_t, t0, n, QcT, tok_off, ropeA=None, halves=(0, 1)):
            for half in halves:
                bk, bt = proj_tm(hT_ap, h_t, t0, n, wino, half * 512, (half + 1) * 512)
                yield
                if ropeA is not None:
                    yield from rope_tm(None, bt, n, 8, 64, ropeA[0], ropeA[1], S.scr,
                                       view3=bk[:n, 0:512].rearrange("p (h d) -> p h d", h=8),
                                       dst3=S.tmbf.ap[:n, 0:512].rearrange("p (h d) -> p h d", h=8), dst_t=S.tmbf.t(),
                                       on_dve=True)
                else:
                    fw.op(act, A.copy, S.tmbf.ap[:n, 0:512], bk[:n, 0:512], reads=[bt], writes=[S.tmbf.t()])
                    yield
                yield from transpose_to(None, QcT.t(tok_off // 128), S.tmbf.ap, S.tmbf.t(), n, 4,
                                        group_fn=lambda c0, g, half=half: [
                                            (QcT.ap[0:64, half * 8 + 2 * c0:half * 8 + 2 * (c0 + g):2, tok_off:tok_off + n], 0, 64),
                                            (QcT.ap[64:128, half * 8 + 2 * c0 + 1:half * 8 + 2 * (c0 + g):2, tok_off:tok_off + n], 64, 128)])

        def odd_kv(src, n, KcT2, Vc, key_off, vtile, outs=None, ropeA=None, valid=None):
            if src[0] == "proj":
                _, hT_ap, h_t, t0 = src
                bk, bt = proj_tm(hT_ap, h_t, t0, n, wino, 1024, 1280)
                yield
                kf = S.out_stage()
                fw.op(act, A.copy, kf.ap[:n, 0:256], bk[:n, 0:256], reads=[bt], writes=[kf.t()])
                yield
                if outs is not None:
                    fw.dma(sp, o_ck[outs:outs + n, :], kf.ap[:n, 0:128], reads=[kf.t()], is_output=True)
                    yield
                    fw.dma(sp, o_cv[outs:outs + n, :], kf.ap[:n, 128:256], reads=[kf.t()], is_output=True)
                    yield
                if ropeA is not None:
                    yield from rope_tm(kf.ap[:n, 0:128], kf.t(), n, 2, 64, ropeA[0], ropeA[1], S.scr)
            else:
                _, kf = src
            fw.op(dve, V.tensor_copy, S.tmbf.ap[:n, 0:256].rearrange("p (k r d) -> p k r d", k=2, r=2),
                  kf.ap[:n, 0:128].rearrange("p (k d) -> p k d", k=2).unsqueeze(2).broadcast_to([n, 2, 2, 64]),
                  reads=[kf.t()], writes=[S.tmbf.t()])
            yield
            yield from transpose_to(None, KcT2.t(key_off // 128), S.tmbf.ap, S.tmbf.t(), n, 2,
                                    group_fn=lambda c0, g: [(KcT2.ap[:, c0:c0 + g, key_off:key_off + n], 0, 128)])
            if valid is None:
                fw.op(act, A.copy, Vc.ap[:n, vtile, :, 0:64], kf.ap[:n, 128:256].rearrange("p (k d) -> p k d", k=2),
                      reads=[kf.t()], writes=[Vc.t(vtile)])
                yield
            else:
                fw.op(dve, V.tensor_scalar_mul, Vc.ap[:n, vtile, :, 0:64], kf.ap[:n, 128:256].rearrange("p (k d) -> p k d", k=2),
                      valid, reads=[kf.t(), consts.t()], writes=[Vc.t(vtile)])
                yield
                fw.op(dve, V.tensor_scalar_mul, Vc.ap[:n, vtile, :, 64:128], Vc.ap[:n, vtile, :, 64:128], valid,
                      reads=[Vc.t(vtile), consts.t()], writes=[Vc.t(vtile)])
                yield

        QcT = ar.alloc("QcTp", [16, TP], BF16)
        fw.op(dve, V.memset, QcT.ap, 0.0, writes=alltk(QcT, 8))
        KcT2 = ar.alloc("KcT2p", [2, TP], BF16)
        Vc = ar.alloc("Vcp", [4, 2, 128], BF16)
        ones_cols(Vc, 4, 2)
        def chain_p1q(ti, t0, n, half, QcT=QcT):
            yield from odd_q(hTp.ap, hTp.t(), t0, n, QcT, t0, halves=(half,))

        def chain_p1k(ti, t0, n, KcT2=KcT2, Vc=Vc):
            yield from odd_kv(("proj", hTp.ap, hTp.t(), t0), n, KcT2, Vc, t0, ti, outs=t0)
        gl_ = []
        for ti, (t0, n) in enumerate(ttiles(TP)):
            gl_ += [chain_p1q(ti, t0, n, 0), chain_p1q(ti, t0, n, 1), chain_p1k(ti, t0, n)]
        run_lanes(gl_, width=4)
        for h in range(16):
            kvh = h // 8
            pb = (h % 2) * 64
            for s in range(2):
                q0 = s * SEQ
                attend(SEQ,
                       lambda ki, kvh=kvh, pb=pb, h=h, q0=q0: (
                           [(KcT2.ap[:, kvh, q0 + ki * 128:q0 + ki * 128 + 128],
                             QcT.ap[:, h, q0:q0 + SEQ])], [KcT2.t((q0 + ki * 128) // 128)]),
                       [128, 128],
                       lambda ki, kvh=kvh, s=s: (Vc.ap[:, s * 2 + ki, kvh, :], [Vc.t(s * 2 + ki)]),
                       SC_C, OT.ap[pb:pb + 64, h // 2, q0:q0 + SEQ], OT.t(h // 2), alltk(QcT, 8), sink_col=h)
        flush_attends()
        for b_ in (QcT, KcT2, Vc, hTp):
            b_.free()
        wouto = load_w("wouto", d_wouto, 8, 1024)

        if STAGE >= 2:
            hTs = ar.alloc("h1Ts", [8, TS], BF16)
            drain(norm_mod(xTs.ap, xTs.t(), TS, 1, 0, 1, hTs.ap, hTs.t()))
            ropeA_q1 = ar.alloc("ropeA_q1", [5, 64], F32)
            ropeA_k1 = ar.alloc("ropeA_k1", [NK1T, 64], F32)
            fw.dma(sp, ropeA_q1.ap, d_ropeA_q1, writes=[ropeA_q1.t()])
            fw.dma(sp, ropeA_k1.ap, d_ropeA_k1, writes=[ropeA_k1.t()])
            band = ar.alloc("band", [NK1T, TQ1], BF16)
            fw.dma(pool, band.ap, d_band, writes=[band.t()])
            NKC = NK1T * K1T
            KcT2 = ar.alloc("KcT2s", [2, NKC + NCTX], BF16)
            Vc = ar.alloc("Vcs", [NK1T + 4, 2, 128], BF16)
            ones_cols(Vc, NK1T + 4, 2)
            QcT = ar.alloc("QcTs", [16, TQ1], BF16)
            fw.op(dve, V.memset, QcT.ap, 0.0, writes=alltk(QcT, 8))
            def chain_k1(jt, KcT2=KcT2, Vc=Vc):
                t0 = K1OFF + jt * K1T
                yield from odd_kv(("proj", hTs.ap, hTs.t(), t0), K1T, KcT2, Vc, jt * K1T, jt,
                                  ropeA=(ropeA_k1.ap[:K1T, jt, :], ropeA_k1.t()), valid=cst("validk1", jt, jt + 1)[:K1T, :])

            def chain_c1(ti, KcT2=KcT2, Vc=Vc):
                kf = S.out_stage()
                r0 = ti * 128
                fw.dma(sp, kf.ap[:, 0:128], d_cck[r0:r0 + 128, :], writes=[kf.t()])
                fw.dma(sp, kf.ap[:, 128:256], d_ccv[r0:r0 + 128, :], writes=[kf.t()])
                yield from odd_kv(("cache", kf), 128, KcT2, Vc, NKC + r0, NK1T + ti)

            def chain_q1(ti, q0, n, half, QcT=QcT):
                yield from odd_q(hTs.ap, hTs.t(), Q1OFF + q0, n, QcT, q0, ropeA=(ropeA_q1.ap[:n, ti, :], ropeA_q1.t()),
                                 halves=(half,))
            run_lanes(width=4, gens=[chain_k1(jt) for jt in range(NK1T)] + [chain_c1(ti) for ti in range(4)] +
                      [chain_q1(ti, q0, n, hf) for ti, (q0, n) in enumerate(ttiles(TQ1)) for hf in (0, 1)])
            hTs.free()
            halves = [((0, 257), [0, 1, 2, 3, 4]), ((257, 514), [2, 3, 4, 5, 6])]
            for h in range(16):
                kvh = h // 8
                pb = (h % 2) * 64
                for (qa, qb), jts in halves:
                    kl = [("b", jt) for jt in jts] + [("c", ci) for ci in range(4)]

                    def s_pairs(ki, kl=kl, kvh=kvh, pb=pb, h=h, qa=qa, qb=qb):
                        kind, idx = kl[ki]
                        if kind == "b":
                            kap = KcT2.ap[:, kvh, idx * K1T:(idx + 1) * K1T]
                            ktk = KcT2.t((idx * K1T) // 128)
                        else:
                            kap = KcT2.ap[:, kvh, NKC + idx * 128:NKC + (idx + 1) * 128]
                            ktk = KcT2.t((NKC + idx * 128) // 128)
                        return [(kap, QcT.ap[:, h, qa:qb])], [ktk]

                    def v_f(ki, kl=kl, kvh=kvh):
                        kind, idx = kl[ki]
                        if kind == "b":
                            return Vc.ap[:K1T, idx, kvh, :], [Vc.t(idx)]
                        return Vc.ap[:, NK1T + idx, kvh, :], [Vc.t(NK1T + idx)]

                    def m_f(ki, kl=kl, qa=qa, qb=qb):
                        kind, idx = kl[ki]
                        if kind == "b":
                            return band.ap[:K1T, idx, qa:qb], band.t()
                        return None

                    attend(qb - qa, s_pairs, [K1T if k[0] == "b" else 128 for k in kl], v_f, SC_C,
                           OT.ap[pb:pb + 64, h // 2, TP + qa:TP + qb], OT.t(h // 2), alltk(QcT, 8), mask_fn=m_f, sink_col=h)
            flush_attends()
            for b_ in (QcT, KcT2, Vc, band, ropeA_q1, ropeA_k1):
                b_.free()
        wino.free()

        xs1 = xTs.ap[:, :, Q1OFF:Q1OFF + TQ1]
        gp = dict(T=TP, ooff=0, xT=xTp.ap, xt=xTp.t(), j=0)
        gs = dict(T=TQ1, ooff=TP, xT=xs1, xt=xTs.t(), j=1)
        out_proj(1, wouto, OT, [gp] + ([gs] if STAGE >= 2 else []))
        wouto.free()
        OT.free()

        ffn_prefetch(1)
        hTp = ar.alloc("h2Tp1", [8, TP], BF16)
        ng_ = [norm_mod(xTp.ap, xTp.t(), TP, 1, 1, 0, hTp.ap, hTp.t())]
        fg = [dict(hT=hTp.ap, ht=hTp.t(), T=TP, xT=xTp.ap, xt=xTp.t(), j=0, segs=[(0, SEQ), (SEQ, 2 * SEQ)])]
        if STAGE >= 2:
            hTs = ar.alloc("h2Ts1", [8, TQ1], BF16)
            ng_.append(norm_mod(xs1, xTs.t(), TQ1, 1, 1, 1, hTs.ap, hTs.t(), mask=vmask.ap[:, Q1OFF:Q1OFF + TQ1]))
        run_lanes(ng_)
        if STAGE >= 2:
            fg.append(dict(hT=hTs.ap, ht=hTs.t(), T=TQ1, xT=xs1, xt=xTs.t(), j=1, segs=[(0, TQ1)]))
        ffn(1, fg)
        hTp.free()
        if STAGE >= 2:
            hTs.free()

        def final_out(xsrc, xt, T, d_out):
            yT = ar.alloc(uniq("yT"), [8, T], F32)
            yield from norm_mod(xsrc, xt, T, 0, 0, 0, yT.ap, yT.t(), plain_gain=cst("gfin"))
            ring = [ar.alloc(uniq("yout"), [1024], F32) for i in range(2)]
            for it, (t0, n) in enumerate(ttiles(T)):
                yb = ring[it % 2]
                for g in range(2):
                    bk, bt = bank("s")
                    for c4 in range(4):
                        c = g * 4 + c4
                        fw.op(pe, nc.tensor.transpose, bk[:n, c4 * 128:(c4 + 1) * 128], yT.ap[:, c, t0:t0 + n],
                              ident.ap, reads=[yT.t(), ident.t()], writes=[bt])
                    evac_copy(yb.ap[:n, g * 512:(g + 1) * 512], bk[:n, :], [bt], [yb.t()])
                    yield
                fw.dma(sp, d_out[t0:t0 + n, :], yb.ap[:n, :], reads=[yb.t()], is_output=True)
                yield
            for b_ in ring:
                b_.free()
            yT.free()

        fo_ = [final_out(xTp.ap, xTp.t(), TP, o_yp)]
        if STAGE >= 2:
            fo_.append(final_out(xTs.ap[:, :, OWNOFF:OWNOFF + 512], xTs.t(), 512, o_ys))
        run_lanes(fo_)
        fw.finish()
        build_program.stats = dict(peak=ar.peak, nsem=fw.nsem,
                                   ninst={e.name: e.ninst for e in (pe, act, dve, pool, sp)})
    return nc


def _prep_shared(inp):
    f = lambda k: np.asarray(inp[k], np.float32)
    sh = {}
    wm = f("w_mod").reshape(2, 8, 128, 12, 512)
    sh["w_mod_r"] = np.ascontiguousarray(wm.transpose(0, 3, 2, 1, 4))
    sh["w_in_e"] = np.ascontiguousarray(f("w_in_e")[0])
    sh["w_uq"] = np.ascontiguousarray(f("w_uq_b")[0])
    wukv = f("w_ukv_b")[0].reshape(256, 8, 128)
    sh["w_ukv"] = np.ascontiguousarray(np.concatenate([wukv[:, :, :64].reshape(256, 512), wukv[:, :, 64:].reshape(256, 512)], axis=1))
    sh["w_out_e"] = np.ascontiguousarray(f("w_out_e")[0])
    sh["w_in_o"] = np.ascontiguousarray(f("w_in_o")[0])
    sh["w_out_o"] = np.ascontiguousarray(f("w_out_o")[0])
    wup = f("w_up")
    g = wup[:, :, :DFF].reshape(2, 8, 128, NJ, 128)
    v = wup[:, :, DFF:].reshape(2, 8, 128, NJ, 128)
    gv = np.concatenate([g, v], axis=-1)
    sh["w_up_r"] = np.ascontiguousarray(gv.transpose(0, 3, 2, 1, 4))
    wdn = f("w_down").reshape(2, NJ, 128, 8, 128)
    sh["w_down_r"] = np.ascontiguousarray(wdn.transpose(0, 3, 2, 1, 4))
    p = np.arange(128)[:, None, None]
    jt = np.arange(NK1T)[None, :, None]
    qi = np.arange(TQ1)[None, None, :]
    d = K1T * jt + p - 128 - qi
    band = ((np.abs(d) <= 128) & (p < K1T)).astype(np.float32)
    sh["band"] = np.ascontiguousarray(band)
    full_pos = np.arange(NFULL)
    sh["ropeA_full"] = _tile_rows(_rope_tables(full_pos, 64), 128, 16)
    sh["ropeB_full"] = _tile_rows(_rope_tables(full_pos, 32), 128, 16)
    return sh


def _prep_core(inp, i):
    f = lambda k: np.asarray(inp[k], np.float32)
    sb, r = i // 4, i % 4
    s = 512 * r
    m = {}
    m["xp"] = np.ascontiguousarray(f("x_prompt")[2 * i:2 * i + 2].reshape(TP, 1024))
    pos = np.arange(s - HALO, s - HALO + TS)
    ok = (pos >= 0) & (pos < NFULL)
    xs = np.zeros((TS, 1024), np.float32)
    xs[ok] = f("x_sample")[sb][pos[ok]]
    m["xs"] = xs
    m["xf"] = np.ascontiguousarray(f("x_sample")[sb])
    m["ca_k"] = np.ascontiguousarray(f("cache_a_k")[sb, 0].reshape(NCTX, 128))
    m["ca_v"] = np.ascontiguousarray(f("cache_a_v")[sb, 0].reshape(NCTX, 128))
    m["cb_ckv"] = np.ascontiguousarray(f("cache_b_ckv")[sb, 0])
    m["cb_kpe"] = np.ascontiguousarray(f("cache_b_kpe")[sb, 0])
    m["cc_k"] = np.ascontiguousarray(f("cache_c_k")[sb, 0].reshape(NCTX, 128))
    m["cc_v"] = np.ascontiguousarray(f("cache_c_v")[sb, 0].reshape(NCTX, 128))
    cs = np.zeros((128, NCONST), np.float32)

    def put(name, arr):
        lo, hi = CL[name]
        cs[:, lo:hi] = np.asarray(arr, np.float32).reshape(128, hi - lo)
    for l in range(2):
        put("gmix%d" % l, _fm(f("g_mix_norm")[l], 8))
        put("gffn%d" % l, _fm(f("g_ffn_norm")[l], 8))
        put("bmod%d" % l, _fm(f("b_mod")[l], 48))
        cw = f("conv_w")[l].reshape(3, NJ, 128).transpose(2, 1, 0)
        put("convw%d" % l, np.ascontiguousarray(cw))
        put("convb%d" % l, _fm(f("conv_b")[l], NJ))
    put("gfin", _fm(f("g_final"), 8))
    cond = np.stack([_fm(f("c_ctx"), 8), _fm(f("c")[sb], 8)], axis=-1)
    put("condT", cond)
    put("sink", _bc(f("sink_c")[0]))
    put("gq", _bc(f("g_qnorm_a")[0]))
    put("gk", _bc(f("g_knorm_a")[0]))
    put("gcq", _bc(f("g_cq_b")[0]))
    put("gckv", _bc(f("g_ckv_b")[0]))
    vk = np.zeros((128, NK1T), np.float32)
    for jt in range(NK1T):
        kp = s - HALO + K1OFF + jt * K1T + np.arange(K1T)
        vk[:K1T, jt] = ((kp >= 0) & (kp < NFULL)).astype(np.float32)
    put("validk1", vk)
    m["consts"] = cs
    m["vmask"] = np.ascontiguousarray(np.broadcast_to(ok.astype(np.float32)[None, :], (128, TS)))
    m["ropeA_ext"] = _tile_rows(_rope_tables(pos, 64), 128, 7)
    m["ropeB_ext"] = _tile_rows(_rope_tables(pos, 32), 128, 7)
    m["ropeA_q1"] = _tile_rows(_rope_tables(pos[Q1OFF:Q1OFF + TQ1], 64), 128, 5)
    m["ropeA_k1"] = _tile_rows(_rope_tables(pos[K1OFF:K1OFF + NK1T * K1T], 64), K1T, NK1T)
    return m


_NC_CACHE = {}


def kernel(**inputs):
    if "nc" not in _NC_CACHE:
        _NC_CACHE["nc"] = build_program()
    nc = _NC_CACHE["nc"]
    shared = _prep_shared(inputs)
    in_maps = []
    for i in range(NCORES):
        m = _prep_core(inputs, i)
        m.update(shared)
        in_maps.append(m)
    res = run_bass_kernel_spmd(nc, in_maps, core_ids=list(range(NCORES)))
    R = res.results
    y_p = np.stack([R[i]["y_p"].reshape(2, SEQ, 1024) for i in range(NCORES)]).reshape(16, SEQ, 1024)
    y_s = np.stack([R[i]["y_s"] for i in range(NCORES)]).reshape(2, NFULL, 1024)

    def cat(name, shp):
        return np.stack([R[i][name].reshape((2, SEQ) + shp) for i in range(NCORES)]).reshape((16, 1, SEQ) + shp)
    outs = (y_p, y_s, cat("n_ak", (2, 64)), cat("n_av", (2, 64)), cat("n_ckv", (256,)), cat("n_kpe", (32,)),
            cat("n_ck", (2, 64)), cat("n_cv", (2, 64)))
    return tuple(np.ascontiguousarray(o.astype(np.float32)) for o in outs)
```
